# Optimizing a Trainium2 kernel written in Bass

```python
import math
import jax
import jax.numpy as jnp
from jax import lax
import numpy as np

D_MODEL = 1024
BATCH = 16
SEQ = 256
DEPTH = 2
DEC_BATCH = 4
DEC_SEQ = 4096
PAST_LEN = 256

GRID_W = 64
N_MIXERS = 2
N_GLA_LAYERS = (DEPTH + 1) // 2
N_MLA_LAYERS = DEPTH // 2
D_FF = 2816
MACARON_W = 0.5
N_MOD = 9
EPS = 1e-6

GLA_HEADS = 4
GLA_DK = D_MODEL // 2 // GLA_HEADS
GLA_DV = D_MODEL // GLA_HEADS
GLA_GATE_RANK = 16
GLA_TAU = 16.0
GLA_CHUNK = 64
GLA_QK = GLA_HEADS * GLA_DK
GLA_VV = GLA_HEADS * GLA_DV
GLA_SPLITS = [GLA_QK, 2 * GLA_QK, 2 * GLA_QK + GLA_VV, 2 * GLA_QK + 2 * GLA_VV]
GLA_IN = 2 * GLA_QK + 2 * GLA_VV + 2 * GLA_GATE_RANK

MLA_HEADS = 16
MLA_NOPE = 128
MLA_ROPE = 64
MLA_V = 128
MLA_Q_RANK = 512
MLA_KV_RANK = 256
MLA_IN = MLA_Q_RANK + MLA_KV_RANK + MLA_ROPE
MLA_SCALE = (MLA_NOPE + MLA_ROPE) ** -0.5
Q_BLOCK = 128
ROPE_BASE = 10000.0

kernel_name = 'hybrid_gla_mla_prefix_diffusion_step'


def rms_norm(x, g):
    xf = x.astype(jnp.float32)
    y = xf * lax.rsqrt(jnp.mean(xf * xf, axis=-1, keepdims=True) + EPS)
    return (y * g.astype(jnp.float32)).astype(x.dtype)


def swiglu(h, w_in, w_out):
    g, u = jnp.split(h @ w_in, 2, axis=-1)
    return (jax.nn.silu(g) * u) @ w_out


def modulation(cond, w_mod, b_mod):
    m = jax.nn.silu(cond) @ w_mod + b_mod
    return jnp.split(m[..., None, :], N_MOD, axis=-1)


def axial_rope_tables(n_tokens):
    rows = n_tokens // GRID_W
    row = jnp.repeat(jnp.arange(rows), GRID_W).astype(jnp.float32)
    col = jnp.tile(jnp.arange(GRID_W), rows).astype(jnp.float32)
    n_pairs = MLA_ROPE // 4
    inv = ROPE_BASE ** (-jnp.arange(n_pairs, dtype=jnp.float32) / n_pairs)
    ang = jnp.stack([row[:, None] * inv, col[:, None] * inv], axis=1)
    return jnp.cos(ang), jnp.sin(ang)


def apply_axial_rope(x, cos, sin):
    xs = x.reshape(x.shape[:-1] + (2, 2, MLA_ROPE // 4)).astype(jnp.float32)
    x1, x2 = xs[..., 0, :], xs[..., 1, :]
    out = jnp.stack([x1 * cos - x2 * sin, x2 * cos + x1 * sin], axis=-2)
    return out.reshape(x.shape).astype(x.dtype)


def gla_scan(q, k, v, log_a, h0):
    bsz, t = q.shape[0], q.shape[1]
    n = t // GLA_CHUNK

    def chunks(a):
        return jnp.moveaxis(a.reshape((bsz, n, GLA_CHUNK) + a.shape[2:]), 1, 0)

    lower = jnp.tril(jnp.ones((GLA_CHUNK, GLA_CHUNK), bool))[None, :, :, None, None]

    def step(h, inp):
        qc, kc, vc, gc = inp
        b = jnp.cumsum(gc, axis=1)
        diff = jnp.where(lower, b[:, :, None] - b[:, None, :], -jnp.inf)
        attn = jnp.einsum('bijhd,bjhd->bhij', qc[:, :, None] * jnp.exp(diff), kc)
        o = (jnp.einsum('bhij,bjhv->bihv', attn, vc)
             + jnp.einsum('bihd,bhdv->bihv', qc * jnp.exp(b), h))
        b_last = b[:, -1]
        h = (jnp.exp(b_last)[..., None] * h
             + jnp.einsum('bjhd,bjhv->bhdv', kc * jnp.exp(b_last[:, None] - b), vc))
        return h, o

    h_final, o = lax.scan(step, h0, (chunks(q), chunks(k), chunks(v), chunks(log_a)))
    return jnp.moveaxis(o, 0, 1).reshape(v.shape), h_final


def gla_mixer(h, w_in, w_gate, b_gate, g_out, w_out, state0):
    bsz, t = h.shape[0], h.shape[1]
    f32 = jnp.float32
    q, k, v, r, z = jnp.split(h @ w_in, GLA_SPLITS, axis=-1)
    q = q.reshape(bsz, t, GLA_HEADS, GLA_DK).astype(f32) * (GLA_DK ** -0.5)
    k = k.reshape(bsz, t, GLA_HEADS, GLA_DK).astype(f32)
    v = v.reshape(bsz, t, GLA_HEADS, GLA_DV).astype(f32)
    z = z.reshape(bsz, t, 2, GLA_GATE_RANK)
    logit = jnp.einsum('btdr,drk->btdk', z, w_gate) + b_gate
    log_a = (jax.nn.log_sigmoid(logit.astype(f32)) / GLA_TAU).reshape(bsz, t, 2, GLA_HEADS, GLA_DK)
    o_f, s_f = gla_scan(q, k, v, log_a[:, :, 0], state0[:, 0])
    o_b, s_b = gla_scan(q[:, ::-1], k[:, ::-1], v[:, ::-1], log_a[:, ::-1, 1], state0[:, 1])
    o = rms_norm(o_f + o_b[:, ::-1], g_out.reshape(GLA_HEADS, GLA_DV))
    o = o.reshape(bsz, t, GLA_VV).astype(h.dtype) * jax.nn.silu(r)
    return o @ w_out, jnp.stack([s_f, s_b], axis=1)


def mla_project(h, w_in, g_q, g_kv):
    cq, ckv, kr = jnp.split(h @ w_in, [MLA_Q_RANK, MLA_Q_RANK + MLA_KV_RANK], axis=-1)
    return rms_norm(cq, g_q), rms_norm(ckv, g_kv), kr


def mla_heads(cq, ckv_all, w_uq, w_ukv):
    bsz, tq = cq.shape[0], cq.shape[1]
    tk = ckv_all.shape[1]
    q = (cq @ w_uq).reshape(bsz, tq, MLA_HEADS, MLA_NOPE + MLA_ROPE)
    kv = (ckv_all @ w_ukv).reshape(bsz, tk, MLA_HEADS, MLA_NOPE + MLA_V)
    return q[..., :MLA_NOPE], q[..., MLA_NOPE:], kv[..., :MLA_NOPE], kv[..., MLA_NOPE:]


def mla_attention(q_nope, q_rope, k_nope, k_rope, v):
    bsz, tq = q_nope.shape[0], q_nope.shape[1]
    nb = tq // Q_BLOCK

    def blocks(a):
        return jnp.moveaxis(a.reshape((bsz, nb, Q_BLOCK) + a.shape[2:]), 1, 0)

    def one_block(qs):
        qn, qr = qs
        s = jnp.einsum('bqhd,bkhd->bhqk', qn, k_nope) + jnp.einsum('bqhr,bkr->bhqk', qr, k_rope)
        p = jax.nn.softmax(s.astype(jnp.float32) * MLA_SCALE, axis=-1).astype(v.dtype)
        return jnp.einsum('bhqk,bkhv->bqhv', p, v)

    o = lax.map(one_block, (blocks(q_nope), blocks(q_rope)))
    return jnp.moveaxis(o, 0, 1).reshape(bsz, tq, MLA_HEADS * MLA_V)


def mla_context(h, w_in, g_q, g_kv, w_uq, w_ukv, w_out):
    cq, ckv, kr = mla_project(h, w_in, g_q, g_kv)
    qn, qr, kn, v = mla_heads(cq, ckv, w_uq, w_ukv)
    return mla_attention(qn, qr, kn, kr, v) @ w_out, (ckv, kr)


def mla_latent(h, ckv_ctx, kr_ctx, w_in, g_q, g_kv, w_uq, w_ukv, w_out):
    cq, ckv, kr = mla_project(h, w_in, g_q, g_kv)
    cos, sin = axial_rope_tables(h.shape[1])
    ckv_all = jnp.concatenate([ckv_ctx.astype(ckv.dtype), ckv], axis=1)
    kr_all = jnp.concatenate([kr_ctx.astype(kr.dtype), apply_axial_rope(kr, cos, sin)], axis=1)
    qn, qr, kn, v = mla_heads(cq, ckv_all, w_uq, w_ukv)
    qr = apply_axial_rope(qr, cos[:, None], sin[:, None])
    return mla_attention(qn, qr, kn, kr_all, v) @ w_out


def run_layer(x, mods, g_norm, w_ffn_in, w_ffn_out, mixer):
    def sub(x, i, fn, weight):
        shift, scale, gate = mods[3 * i], mods[3 * i + 1], mods[3 * i + 2]
        h = rms_norm(x, g_norm[i, 0]) * (1.0 + scale) + shift
        y, aux = fn(h)
        return x + weight * gate * rms_norm(y, g_norm[i, 1]), aux

    x, _ = sub(x, 0, lambda h: (swiglu(h, w_ffn_in[0], w_ffn_out[0]), None), MACARON_W)
    x, aux = sub(x, 1, mixer, 1.0)
    x, _ = sub(x, 2, lambda h: (swiglu(h, w_ffn_in[1], w_ffn_out[1]), None), MACARON_W)
    return x, aux


def setup_inputs(seed: int = 0) -> dict:
    key = jax.random.key(seed)
    ks = jax.random.split(key, 32)
    f32 = jnp.float32

    def nrm(k, shape, fan_in, gain=1.0):
        return jax.random.normal(k, shape, f32) * (gain * fan_in ** -0.5)

    def gains(k, shape):
        return 1.0 + 0.05 * jax.random.normal(k, shape, f32)

    D = D_MODEL
    return {
        'x_prompt': jax.random.normal(ks[0], (BATCH, SEQ, D), f32),
        'x_sample': jax.random.normal(ks[1], (DEC_BATCH, DEC_SEQ, D), f32),
        'state_gla': jax.random.normal(ks[2], (DEC_BATCH, N_GLA_LAYERS, 2, GLA_HEADS, GLA_DK, GLA_DV), f32),
        'cache_mla_ckv': jax.random.normal(ks[3], (DEC_BATCH, N_MLA_LAYERS, PAST_LEN, MLA_KV_RANK), f32),
        'cache_mla_krope': jax.random.normal(ks[4], (DEC_BATCH, N_MLA_LAYERS, PAST_LEN, MLA_ROPE), f32),
        'c': jax.random.normal(ks[5], (DEC_BATCH, D), f32),
        'c_ctx': jax.random.normal(ks[6], (D,), f32),
        'w_mod': nrm(ks[7], (DEPTH, D, N_MOD * D), D, 0.5),
        'b_mod': 0.02 * jax.random.normal(ks[8], (DEPTH, N_MOD * D), f32),
        'g_norm': gains(ks[9], (DEPTH, 3, 2, D)),
        'w_ffn_in': nrm(ks[10], (DEPTH, 2, D, 2 * D_FF), D),
        'w_ffn_out': nrm(ks[11], (DEPTH, 2, D_FF, D), D_FF),
        'gla_w_in': nrm(ks[12], (N_GLA_LAYERS, D, GLA_IN), D),
        'gla_w_gate': nrm(ks[13], (N_GLA_LAYERS, 2, GLA_GATE_RANK, GLA_QK), GLA_GATE_RANK),
        'gla_b_gate': 0.1 * jax.random.normal(ks[14], (N_GLA_LAYERS, 2, GLA_QK), f32),
        'gla_g_out': gains(ks[15], (N_GLA_LAYERS, GLA_VV)),
        'gla_w_out': nrm(ks[16], (N_GLA_LAYERS, GLA_VV, D), GLA_VV),
        'mla_w_in': nrm(ks[17], (N_MLA_LAYERS, D, MLA_IN), D),
        'mla_g_q': gains(ks[18], (N_MLA_LAYERS, MLA_Q_RANK)),
        'mla_g_kv': gains(ks[19], (N_MLA_LAYERS, MLA_KV_RANK)),
        'mla_w_uq': nrm(ks[20], (N_MLA_LAYERS, MLA_Q_RANK, MLA_HEADS * (MLA_NOPE + MLA_ROPE)), MLA_Q_RANK),
        'mla_w_ukv': nrm(ks[21], (N_MLA_LAYERS, MLA_KV_RANK, MLA_HEADS * (MLA_NOPE + MLA_V)), MLA_KV_RANK),
        'mla_w_out': nrm(ks[22], (N_MLA_LAYERS, MLA_HEADS * MLA_V, D), MLA_HEADS * MLA_V),
    }


def reference(x_prompt, x_sample, state_gla, cache_mla_ckv, cache_mla_krope, c, c_ctx,
              w_mod, b_mod, g_norm, w_ffn_in, w_ffn_out,
              gla_w_in, gla_w_gate, gla_b_gate, gla_g_out, gla_w_out,
              mla_w_in, mla_g_q, mla_g_kv, mla_w_uq, mla_w_ukv, mla_w_out):
    f32 = jnp.float32
    yp, ys = x_prompt, x_sample
    gla_states, mla_ckvs, mla_krs = [], [], []
    for l in range(DEPTH):
        mods_ctx = modulation(c_ctx, w_mod[l], b_mod[l])
        mods_lat = modulation(c, w_mod[l], b_mod[l])
        j = l // N_MIXERS
        if l % N_MIXERS == 0:
            p = (gla_w_in[j], gla_w_gate[j], gla_b_gate[j], gla_g_out[j], gla_w_out[j])
            zeros = jnp.zeros((yp.shape[0], 2, GLA_HEADS, GLA_DK, GLA_DV), f32)
            yp, st = run_layer(yp, mods_ctx, g_norm[l], w_ffn_in[l], w_ffn_out[l],
                               lambda h: gla_mixer(h, *p, zeros))
            gla_states.append(st)
            s0 = state_gla[:, j].astype(f32)
            ys, _ = run_layer(ys, mods_lat, g_norm[l], w_ffn_in[l], w_ffn_out[l],
                              lambda h: (gla_mixer(h, *p, s0)[0], None))
        else:
            p = (mla_w_in[j], mla_g_q[j], mla_g_kv[j], mla_w_uq[j], mla_w_ukv[j], mla_w_out[j])
            yp, (ckv, kr) = run_layer(yp, mods_ctx, g_norm[l], w_ffn_in[l], w_ffn_out[l],
                                      lambda h: mla_context(h, *p))
            mla_ckvs.append(ckv)
            mla_krs.append(kr)
            ckv_ctx, kr_ctx = cache_mla_ckv[:, j], cache_mla_krope[:, j]
            ys, _ = run_layer(ys, mods_lat, g_norm[l], w_ffn_in[l], w_ffn_out[l],
                              lambda h: (mla_latent(h, ckv_ctx, kr_ctx, *p), None))
    new_state_gla = jnp.stack(gla_states, axis=1).astype(x_prompt.dtype)
    new_cache_mla_ckv = jnp.stack(mla_ckvs, axis=1)
    new_cache_mla_krope = jnp.stack(mla_krs, axis=1)
    return (yp, ys, new_state_gla, new_cache_mla_ckv, new_cache_mla_krope)
```

```python
import os
from contextlib import ExitStack
import numpy as np
import concourse.bass as bass
import concourse.mybir as mybir
from concourse.bass_utils import run_bass_kernel_spmd

F32 = mybir.dt.float32
BF16 = mybir.dt.bfloat16
AF = mybir.ActivationFunctionType
ALU = mybir.AluOpType
AX = mybir.AxisListType

NCORES = 8
D = 1024
NT = 2560
NS = 2048
DFF = 2816
NHC = 22
EPS = 1e-6
ENGS = ("pe", "act", "dve", "pool", "sp")
STAGE = os.environ.get("MK_STAGE", "full")


class Buf:
    __slots__ = ("name", "last_writer", "readers")

    def __init__(self, name):
        self.name = name
        self.last_writer = None
        self.readers = []


class Op:
    __slots__ = ("eng", "fn", "dma", "ndma", "inc", "deps", "signal", "sigval", "semkey", "waits", "alldma", "snap")

    def __init__(self, eng, fn, dma, ndma, inc):
        self.eng = eng
        self.fn = fn
        self.dma = dma
        self.ndma = ndma
        self.inc = inc
        self.deps = {}
        self.signal = False
        self.sigval = None
        self.semkey = None
        self.waits = None
        self.alldma = False
        self.snap = None


class Sched:
    def __init__(self):
        self.ops = []
        self.bufs = {}
        self.nbar = 0

    def buf(self, *key):
        b = self.bufs.get(key)
        if b is None:
            b = Buf(key)
            self.bufs[key] = b
        return b

    def add(self, eng, fn, reads=(), writes=(), dma=False, ndma=1, inc=16, semkey=None):
        op = Op(eng, fn, dma, ndma, inc)
        for b in reads:
            w = b.last_writer
            if w is not None:
                op.deps[w] = True
        for b in writes:
            w = b.last_writer
            if w is not None and w not in op.deps:
                op.deps[w] = False
            for r in b.readers:
                if r is not op and r not in op.deps:
                    op.deps[r] = False
        for b in reads:
            b.readers.append(op)
        for b in writes:
            b.last_writer = op
            b.readers = []
        if dma:
            op.semkey = semkey if semkey is not None else ("dma", writes[0].name)
        self.ops.append(op)
        return op

    def finalize(self):
        for op in self.ops:
            need = []
            for d, raw in op.deps.items():
                if d.dma or op.dma:
                    need.append(d)
                elif d.eng != op.eng:
                    need.append(d)
                elif op.eng != "pe":
                    need.append(d)
            op.waits = need
            for d in need:
                d.signal = True
        counts = {}
        for op in self.ops:
            if op.alldma:
                op.snap = {k: v for k, v in counts.items() if k[0] != "eng"}
            if op.dma:
                k = op.semkey
                counts[k] = counts.get(k, 0) + op.inc * op.ndma
                op.sigval = counts[k]
                op.signal = True
            elif op.signal:
                k = ("eng", op.eng)
                op.semkey = k
                counts[k] = counts.get(k, 0) + 1
                op.sigval = counts[k]
        self.final_counts = counts

    def emit(self, nc, final_wait_keys=()):
        counts = self.final_counts
        keys = list(counts.keys())
        with ExitStack() as es:
            sems = {}
            for i, k in enumerate(keys):
                sems[k] = es.enter_context(nc.semaphore("s%d" % i))
            block = es.enter_context(nc.Block())
            by_eng = {e: [] for e in ENGS}
            for op in self.ops:
                by_eng[op.eng].append(op)

            def run(eng_name, eng):
                waited = {}
                for op in by_eng[eng_name]:
                    wmax = {}
                    for d in op.waits:
                        k = d.semkey
                        if d.sigval > wmax.get(k, 0):
                            wmax[k] = d.sigval
                    if op.snap:
                        for k, v in op.snap.items():
                            if v > wmax.get(k, 0):
                                wmax[k] = v
                    for k, v in wmax.items():
                        if waited.get(k, 0) >= v:
                            continue
                        eng.wait_ge(sems[k], v)
                        waited[k] = v
                    r = op.fn(eng)
                    if op.dma:
                        rl = r if isinstance(r, (list, tuple)) else [r]
                        assert len(rl) == op.ndma, (len(rl), op.ndma)
                        for ins in rl:
                            ins.then_inc(sems[op.semkey], op.inc)
                    elif op.signal:
                        r.then_inc(sems[op.semkey], 1)
                if eng_name == "sp":
                    for k in final_wait_keys:
                        if k in counts:
                            eng.wait_ge(sems[k], counts[k])

            @block.tensor
            def _(e):
                run("pe", e)

            @block.scalar
            def _(e):
                run("act", e)

            @block.vector
            def _(e):
                run("dve", e)

            @block.gpsimd
            def _(e):
                run("pool", e)

            @block.sync
            def _(e):
                run("sp", e)


class Arena:
    def __init__(self, t, nwords):
        self.t = t
        self.n = nwords
        self.off = 0
        self.peak = 0

    def mark(self):
        return self.off

    def reset(self, m):
        self.off = m

    def alloc(self, shape, dt, parts=128):
        n = int(np.prod(shape))
        words = (n + 1) // 2 if dt == BF16 else n
        words = (words + 7) // 8 * 8
        o = self.off
        self.off += words
        self.peak = max(self.peak, self.off)
        assert self.off <= self.n, ("arena overflow", self.off, self.n)
        v = self.t[0:parts, o:o + words]
        if dt == BF16:
            v = v.bitcast(BF16)
        v = v[:, 0:n]
        if len(shape) == 2:
            v = v.rearrange("p (a b) -> p a b", b=shape[1])
        elif len(shape) == 3:
            v = v.rearrange("p (a b c) -> p a b c", b=shape[1], c=shape[2])
        return v


def build_program():
    nc = bass.Bass("TRN2", target_bir_lowering=False)

    def din(name, shape, dt=F32):
        return nc.dram_tensor(name, list(shape), dt, kind="ExternalInput").ap()

    def dout(name, shape, dt=F32):
        return nc.dram_tensor(name, list(shape), dt, kind="ExternalOutput").ap()

    def dscr(name, shape, dt=F32):
        return nc.dram_tensor(name, list(shape), dt, kind="Internal").ap()

    xT_d = din("xT", [128, 8, NT])
    cond_d = din("condT", [128, 8, 2])
    wmod_d = din("wmod", [2, 9, 128, 8, 1024])
    bmod_d = din("bmodT", [128, 2, 72])
    gn_d = din("gnT", [128, 2, 3, 2, 8])
    w1_d = din("w1", [2, 2, NHC, 128, 8, 256])
    w2_d = din("w2", [2, 2, 8, 128, NHC, 128])
    gwin_d = din("gwin", [128, 8, 3104])
    gwg_d = din("gwg", [128, 2, 512])
    ggo_d = din("ggo", [128, 1024])
    gwo_d = din("gwo", [8, 128, 8, 128])
    st0_d = din("st0", [128, 4, 256])
    sel_d = din("sel", [128, 2])
    tri_d = din("tri", [128, 4, 128])
    msk_d = din("msk", [128, 2, 128])
    ident_d = din("ident", [128, 128])
    mwin_d = din("mwin", [128, 8, 896])
    mgq_d = din("mgq", [128, 4])
    mgkv_d = din("mgkv", [128, 2])
    wuq_d = din("wuq", [16, 128, 4, 256])
    wukv_d = din("wukv", [16, 128, 2, 256])
    mwo_d = din("mwo", [8, 128, 16, 128])
    cos_d = din("cosT", [64, NS])
    sin_d = din("sinT", [64, NS])
    cckv_d = din("cckv", [128, 2, 256])
    ckr_d = din("ckr", [64, 256])

    yT_d = dout("yT", [128, 8, NT])
    stout_d = dout("stout", [2, 2, 128, 4, 256])
    ckvout_d = dout("ckvout", [128, 2, 512])
    krout_d = dout("krout", [64, 512])

    stscr_d = dscr("stscr", [20, 128, 1024], BF16)
    cc1i_d = dscr("cc1i", [128, 1024])
    cc1o_d = dscr("cc1o", [256, 1024])
    cc2i_d = dscr("cc2i", [320, 2048], BF16)
    cc2o_d = dscr("cc2o", [640, 2048], BF16)
    kvc_d = dscr("kvc", [16, 128, 8704], BF16)
    RG = [[0, 1], [2, 3], [4, 5], [6, 7]]

    S = Sched()
    B = S.buf
    es = ExitStack()
    NW = 53200
    arena_t = es.enter_context(nc.sbuf_tensor("arena", [128, NW], F32))
    A = Arena(arena_t, NW)
    ps = [es.enter_context(nc.psum_tensor("ps%d" % i, [128, 512], F32))[:] for i in range(8)]
    PB = [B("ps", i) for i in range(8)]

    def DMA(eng, out, in_, reads, writes, semkey=None):
        S.add(eng, lambda e, o=out, i=in_: e.dma_start(out=o, in_=i), reads=reads, writes=writes, dma=True, semkey=semkey)

    def ACT(out, in_, func, reads, writes, **kw):
        S.add("act", lambda e, o=out, i=in_, f=func, k=kw: e.activation(out=o, in_=i, func=f, **k), reads=reads, writes=writes)

    def ACOPY(out, in_, reads, writes):
        S.add("act", lambda e, o=out, i=in_: e.copy(out=o, in_=i), reads=reads, writes=writes)

    def VCOPY(out, in_, reads, writes):
        S.add("dve", lambda e, o=out, i=in_: e.tensor_copy(out=o, in_=i), reads=reads, writes=writes)

    def TT(out, in0, in1, op, reads, writes):
        S.add("dve", lambda e, o=out, a=in0, b=in1, p=op: e.tensor_tensor(out=o, in0=a, in1=b, op=p), reads=reads, writes=writes)

    def STT(out, in0, scalar, in1, op0, op1, reads, writes):
        S.add("dve", lambda e, o=out, a=in0, s=scalar, b=in1, p0=op0, p1=op1:
              e.scalar_tensor_tensor(out=o, in0=a, scalar=s, in1=b, op0=p0, op1=p1), reads=reads, writes=writes)

    def TS(out, in0, s1, op0, reads, writes):
        S.add("dve", lambda e, o=out, a=in0, s=s1, p=op0: e.tensor_scalar(out=o, in0=a, scalar1=s, scalar2=None, op0=p), reads=reads, writes=writes)

    def MMG(out, pairs, reads, writes):
        def fn(e, o=out, pr=pairs):
            n = len(pr)
            ins = None
            for i, (l, r) in enumerate(pr):
                ins = e.matmul(o, lhsT=l, rhs=r, start=(i == 0), stop=(i == n - 1))
            return ins
        S.add("pe", fn, reads=reads, writes=writes)

    def MM1(out, l, r, start, stop, reads, writes):
        S.add("pe", lambda e, o=out, a=l, b=r, s0=start, s1=stop: e.matmul(o, lhsT=a, rhs=b, start=s0, stop=s1), reads=reads, writes=writes)

    def barrier():
        S.nbar += 1
        n = S.nbar
        S.add("pe", lambda e: e.matmul(ps[7][0:1, 0:1], lhsT=onesf[0:1, 0:1], rhs=onesf[0:1, 0:1], start=True, stop=True),
              reads=[B("onesf")], writes=[PB[7], B("bar", "pe")])
        S.add("act", lambda e: e.copy(out=bscr[:, 0:1], in_=bscr[:, 4:5]), writes=[B("bscr", 0), B("bar", "act")])
        S.add("dve", lambda e: e.tensor_copy(out=bscr[:, 1:2], in_=bscr[:, 5:6]), writes=[B("bscr", 1), B("bar", "dve")])
        S.add("pool", lambda e: e.tensor_copy(out=bscr[:, 2:3], in_=bscr[:, 6:7]), writes=[B("bscr", 2), B("bar", "pool")])
        allb = [B("bar", x) for x in ("pe", "act", "dve", "pool")]
        o = S.add("pe", lambda e: e.matmul(ps[7][0:1, 0:1], lhsT=onesf[0:1, 0:1], rhs=onesf[0:1, 0:1], start=True, stop=True),
                  reads=allb + [B("onesf")], writes=[PB[7]])
        o.alldma = True
        o = S.add("act", lambda e: e.copy(out=bscr[:, 0:1], in_=bscr[:, 4:5]), reads=allb, writes=[B("bscr", 0)])
        o.alldma = True
        o = S.add("dve", lambda e: e.tensor_copy(out=bscr[:, 1:2], in_=bscr[:, 5:6]), reads=allb, writes=[B("bscr", 1)])
        o.alldma = True
        o = S.add("pool", lambda e: e.tensor_copy(out=bscr[:, 2:3], in_=bscr[:, 6:7]), reads=allb, writes=[B("bscr", 2)])
        o.alldma = True
        o = S.add("sp", lambda e: None, reads=[], writes=[])
        o.deps = {b.last_writer: True for b in allb}
        o.alldma = True

    def xb(k, tok0, n):
        return [B("x", k, j) for j in range(tok0 // 128, (tok0 + n) // 128)]

    xT = A.alloc((8, NT), F32)
    onesM = A.alloc((128,), BF16)
    onesf = A.alloc((128,), F32)
    negcol = A.alloc((1,), F32)
    epsb = A.alloc((1,), F32)
    bscr = A.alloc((8,), F32)
    identb = A.alloc((128,), BF16)
    mods = [A.alloc((72, 2), F32) for _ in range(2)]
    Asc = [[A.alloc((8, 2), F32) for _ in range(3)] for _ in range(2)]
    Gsc = [[A.alloc((8, 2), F32) for _ in range(3)] for _ in range(2)]
    gn = A.alloc((2, 3, 2, 8), F32) if False else None
    gn = A.alloc((96,), F32).rearrange("p (l s j k) -> p l s j k", l=2, s=3, j=2, k=8)
    sq_ring = [A.alloc((512,), BF16) for _ in range(2)]
    rs_buf = A.alloc((512,), F32)
    tmp_ring = [A.alloc((512,), F32) for _ in range(2)]
    PERSIST = A.mark()
    ctr = {"sq": 0, "tmp": 0}

    for k in range(8):
        DMA("sp", xT[:, k, :], xT_d[:, k, :], [], xb(k, 0, NT))
    S.add("dve", lambda e: e.memset(onesM, 1.0 / 1024.0), writes=[B("onesM")])
    S.add("dve", lambda e: e.memset(onesf, 1.0), writes=[B("onesf")])
    S.add("dve", lambda e: e.memset(negcol, -1.0 / 16.0), writes=[B("negcol")])
    S.add("dve", lambda e: e.memset(epsb, EPS), writes=[B("eps")])
    S.add("dve", lambda e: e.memset(bscr, 0.0), writes=[B("bscr", i) for i in range(4)])
    DMA("sp", gn.rearrange("p l s j k -> p (l s j k)"), gn_d.rearrange("p l s j k -> p (l s j k)"), [], [B("gn")])

    m0 = A.mark()
    cond_f = A.alloc((8, 2), F32)
    scb = A.alloc((8, 2), BF16)
    bmod = A.alloc((2, 72), F32)
    identf = A.alloc((128,), F32)
    wm_slots = [A.alloc((8, 1024), BF16) for _ in range(2)]
    DMA("sp", cond_f, cond_d, [], [B("cond_f")])
    DMA("sp", bmod, bmod_d, [], [B("bmod")])
    DMA("sp", identf, ident_d, [], [B("identf")])
    ACOPY(identb, identf, [B("identf")], [B("ident")])
    ACT(scb, cond_f, AF.Silu, [B("cond_f")], [B("scb")])
    psM = ps[6][:, 0:144].rearrange("p (a b) -> p a b", b=2)
    for l in range(2):
        for i in range(9):
            si = (l * 9 + i) % 2
            slot = wm_slots[si]
            DMA("pool", slot, wmod_d[l, i], [], [B("wm", si)])
            for jc in range(8):
                col = i * 8 + jc
                MMG(psM[:, col, :], [(slot[:, kc, jc * 128:(jc + 1) * 128], scb[:, kc, :]) for kc in range(8)],
                    [B("wm", si), B("scb")], [PB[6]])
        for c in range(2):
            TT(mods[l][:, :, c], psM[:, :, c], bmod[:, l, :], ALU.add, [PB[6], B("bmod")], [B("mods", l)])
        for s in range(3):
            w = 1.0 if s == 1 else 0.5
            for c in range(2):
                STT(Asc[l][s][:, :, c], mods[l][:, (3 * s + 1) * 8:(3 * s + 2) * 8, c], 1.0, gn[:, l, s, 0, :], ALU.add, ALU.mult,
                    [B("mods", l), B("gn")], [B("Asc", l, s)])
                STT(Gsc[l][s][:, :, c], mods[l][:, (3 * s + 2) * 8:(3 * s + 3) * 8, c], w, gn[:, l, s, 1, :], ALU.mult, ALU.mult,
                    [B("mods", l), B("gn")], [B("Gsc", l, s)])
    barrier()
    A.reset(m0)

    def stats_sq(src, srcbufs, n, slot=None, own=None):
        if own is not None:
            sq, sqb = own
        elif slot is not None:
            sq, sqb = sq_ring[slot], B("sq", slot)
        else:
            sq = sq_ring[ctr["sq"] % 2]
            sqb = B("sq", ctr["sq"] % 2)
            ctr["sq"] += 1
        ACT(sq[:, 0:n], src, AF.Square, srcbufs, [sqb])
        return sq, sqb

    def stats_mm(sq, sqb, n, bank, first, last):
        MM1(ps[bank][:, 0:n], onesM, sq[:, 0:n], first, last, [B("onesM"), sqb], [PB[bank]])

    def stats2(bank, n, scale, rs, rskey):
        ACT(rs[:, 0:n], ps[bank][:, 0:n], AF.Ln, [PB[bank], B("eps")], [B(*rskey)], bias=epsb[:, 0:1], scale=scale)
        ACT(rs[:, 0:n], rs[:, 0:n], AF.Exp, [B(*rskey)], [B(*rskey)], scale=-0.5)

    def stats(src_of_k, nk, n, srcbufs_of_k, scale):
        for k in range(nk):
            sq, sqb = stats_sq(src_of_k(k), srcbufs_of_k(k), n)
            stats_mm(sq, sqb, n, 7, k == 0, k == nk - 1)
        stats2(7, n, scale, rs_buf, ("rs",))

    def prenorm(h, hkey, tok0, n, l, s, c, rs=None, rskey=("rs",)):
        if rs is None:
            stats(lambda k: xT[:, k, tok0:tok0 + n], 8, n, lambda k: xb(k, tok0, n), 1.0)
            rs = rs_buf
        for k in range(8):
            ti = ctr["tmp"] % 2
            ctr["tmp"] += 1
            tmp = tmp_ring[ti]
            TT(tmp[:, 0:n], xT[:, k, tok0:tok0 + n], rs[:, 0:n], ALU.mult, xb(k, tok0, n) + [B(*rskey)], [B("tmp", ti)])
            ACT(h[:, k, 0:n], tmp[:, 0:n], AF.Identity, [B("tmp", ti), B("mods", l), B("Asc", l, s)],
                hkey(k) if callable(hkey) else [B(hkey, k)],
                bias=mods[l][:, 3 * s * 8 + k, c:c + 1], scale=Asc[l][s][:, k, c:c + 1])

    def postnorm_residual(ysb, ykey, tok0, n, l, s, c, stats_bank=None):
        yb = ykey if callable(ykey) else (lambda k: [B(ykey, k)])
        if stats_bank is None:
            stats(lambda k: ysb[:, k, 0:n], 8, n, yb, 1.0)
        else:
            stats2(stats_bank, n, 1.0, rs_buf, ("rs",))
        for k in range(8):
            ti = ctr["tmp"] % 2
            ctr["tmp"] += 1
            tmp = tmp_ring[ti]
            STT(tmp[:, 0:n], ysb[:, k, 0:n], Gsc[l][s][:, k, c:c + 1], rs_buf[:, 0:n], ALU.mult, ALU.mult,
                yb(k) + [B("Gsc", l, s), B("rs")], [B("tmp", ti)])
            S.add("pool", lambda e, o=xT[:, k, tok0:tok0 + n], t=tmp[:, 0:n]: e.tensor_tensor(out=o, in0=o, in1=t, op=ALU.add),
                  reads=xb(k, tok0, n) + [B("tmp", ti)], writes=xb(k, tok0, n))

    def ffn(l, f):
        s = 0 if f == 0 else 2
        m = A.mark()
        ysb = A.alloc((8, 1024), F32)
        h = A.alloc((8, 1024), BF16)
        act = A.alloc((NHC, 1024), BF16)
        w1s = [A.alloc((8, 256), BF16) for _ in range(2)]
        w2s = [A.alloc((NHC, 128), BF16) for _ in range(2)]
        sg = [A.alloc((512,), F32) for _ in range(2)]
        sq_post = (A.alloc((512,), BF16), B("sqp"))
        c1 = 0
        c2 = 0
        c3 = 0
        tiles = ((0, 1024, 0), (1024, 1024, 0), (2048, 512, 1))

        def pre_write_thunks(tok0, sub, c):
            out = []
            t0 = tok0 + sub * 512
            for k in range(8):
                def th(k=k, t0=t0, sub=sub, c=c):
                    ti = ctr["tmp"] % 2
                    ctr["tmp"] += 1
                    tmp = tmp_ring[ti]
                    TT(tmp, xT[:, k, t0:t0 + 512], sg[sub], ALU.mult, xb(k, t0, 512) + [B("sg", sub)], [B("tmp", ti)])
                    ACT(h[:, k, sub * 512:(sub + 1) * 512], tmp, AF.Identity, [B("tmp", ti), B("mods", l), B("Asc", l, s)], [B("h", k, sub)],
                        bias=mods[l][:, 3 * s * 8 + k, c:c + 1], scale=Asc[l][s][:, k, c:c + 1])
                out.append(th)
            return out

        def pre_stat_thunks(tok0, sub):
            out = []
            t0 = tok0 + sub * 512
            hold = {}
            for k in range(9):
                def th(k=k, t0=t0, sub=sub, hold=hold):
                    if k < 8:
                        hold[k] = stats_sq(xT[:, k, t0:t0 + 512], xb(k, t0, 512), 512, slot=k % 2)
                    if k >= 1:
                        sq, sqb = hold[k - 1]
                        stats_mm(sq, sqb, 512, sub, k - 1 == 0, k - 1 == 7)
                out.append(th)
            out.append(lambda sub=sub: stats2(sub, 512, 1.0, sg[sub], ("sg", sub)))
            return out

        def post_thunks(tok0, sub, c):
            out = []
            t0 = tok0 + sub * 512
            out.append(lambda sub=sub: stats2(6 + sub, 512, 1.0, rs_buf, ("rs",)))
            for k in range(8):
                def th(k=k, t0=t0, sub=sub, c=c):
                    ti = ctr["tmp"] % 2
                    ctr["tmp"] += 1
                    tmp = tmp_ring[ti]
                    STT(tmp, ysb[:, k, sub * 512:(sub + 1) * 512], Gsc[l][s][:, k, c:c + 1], rs_buf, ALU.mult, ALU.mult,
                        [B("y", k, sub), B("Gsc", l, s), B("rs")], [B("tmp", ti)])
                    TT(xT[:, k, t0:t0 + 512], xT[:, k, t0:t0 + 512], tmp, ALU.add, xb(k, t0, 512) + [B("tmp", ti)], xb(k, t0, 512))
                out.append(th)
            return out

        for sub in range(tiles[0][1] // 512):
            prenorm(h[:, :, sub * 512:(sub + 1) * 512], (lambda k, sub=sub: [B("h", k, sub)]), tiles[0][0] + sub * 512, 512, l, s, tiles[0][2])
        pend_post = []
        for ti_, (tok0, n, c) in enumerate(tiles):
            nsub = n // 512
            nxt = tiles[ti_ + 1] if ti_ + 1 < len(tiles) else None
            for hc in range(NHC):
                si = c1 % 2
                c1 += 1
                DMA("pool", w1s[si], w1_d[l, f, hc], [], [B("w1s", si)])
                for sub in range(nsub):
                    gi = c3 % 2
                    c3 += 1
                    hb = [B("h", k, sub) for k in range(8)]
                    hs = slice(sub * 512, (sub + 1) * 512)
                    MMG(ps[gi], [(w1s[si][:, kc, 0:128], h[:, kc, hs]) for kc in range(8)], [B("w1s", si)] + hb, [PB[gi], PB[2 + gi]])
                    MMG(ps[2 + gi], [(w1s[si][:, kc, 128:256], h[:, kc, hs]) for kc in range(8)], [B("w1s", si)] + hb, [PB[2 + gi]])
                    ACT(sg[gi], ps[gi], AF.Silu, [PB[gi]], [B("sg", gi)])
                    TT(act[:, hc, hs], sg[gi], ps[2 + gi], ALU.mult, [B("sg", gi), PB[2 + gi]], [B("act", hc, sub)])
                for _ in range(2):
                    if pend_post:
                        pend_post.pop(0)()
            while pend_post:
                pend_post.pop(0)()
            pre_work = []
            if nxt is not None:
                for sub in range(nxt[1] // 512):
                    pre_work += pre_stat_thunks(nxt[0], sub)
                for sub in range(nxt[1] // 512):
                    pre_work += pre_write_thunks(nxt[0], sub, nxt[2])
            nslots = 8 * nsub
            per_slot = -(-len(pre_work) // max(nslots - 1, 1)) if pre_work else 0
            lag = None
            for oc in range(8):
                si = c2 % 2
                c2 += 1
                DMA("pool", w2s[si], w2_d[l, f, oc], [], [B("w2s", si)])
                for sub in range(nsub):
                    bk = 4 + (c2 * 2 + sub) % 2
                    hs = slice(sub * 512, (sub + 1) * 512)
                    ab = [B("act", hc, sub) for hc in range(NHC)]
                    MMG(ps[bk], [(w2s[si][:, hc, :], act[:, hc, hs]) for hc in range(NHC)], [B("w2s", si)] + ab, [PB[bk]])
                    if lag is not None:
                        stats_mm(*lag)
                    ACOPY(ysb[:, oc, hs], ps[bk], [PB[bk]], [B("y", oc, sub)])
                    sq, sqb = stats_sq(ps[bk], [PB[bk]], 512, own=sq_post)
                    lag = (sq, sqb, 512, 6 + sub, oc == 0, oc == 7)
                    for _ in range(per_slot):
                        if pre_work:
                            pre_work.pop(0)()
            stats_mm(*lag)
            while pre_work:
                pre_work.pop(0)()
            for sub in range(nsub):
                pend_post += post_thunks(tok0, sub, c)
        while pend_post:
            pend_post.pop(0)()
        barrier()
        A.reset(m)

    def gla():
        m = A.mark()
        gw = A.alloc((8, 3104), BF16)
        gwo = A.alloc((8, 1024), BF16)
        ggo = A.alloc((1024,), BF16)
        wg = A.alloc((2, 512), F32)
        tri = A.alloc((4, 128), F32)
        msk = A.alloc((2, 128), BF16)
        sel = A.alloc((2,), F32)
        hc_ = A.alloc((8, 128), BF16)
        ksb = A.alloc((512,), BF16)
        ksT = A.alloc((512,), BF16)
        qsT = A.alloc((512,), BF16)
        vtok = A.alloc((1024,), BF16)
        rg = A.alloc((1024,), BF16)
        zT = A.alloc((128,), F32)
        Lb_ = A.alloc((512,), F32)
        L = [Lb_, Lb_]
        khat = A.alloc((512,), BF16)
        dec = A.alloc((4,), F32)
        Sst = A.alloc((4, 256), F32)
        Sb = [A.alloc((1024,), BF16) for _ in range(2)]
        SFb = [A.alloc((1024,), BF16) for _ in range(2)]
        E = A.alloc((512,), F32)
        Ei = A.alloc((512,), F32)
        Ex = Ei
        qtil = [A.alloc((512,), BF16) for _ in range(2)]
        ktil = [A.alloc((512,), BF16) for _ in range(2)]
        Am = [A.alloc((512,), BF16) for _ in range(2)]
        ssq = A.alloc((4,), F32)
        rso = A.alloc((4,), F32)
        ono = A.alloc((256,), BF16)
        og = A.alloc((1024,), BF16)
        ogT = A.alloc((8, 128), BF16)
        ysb = A.alloc((8, 128), F32)
        ysf = ysb.rearrange("p a t -> p (a t)")
        sqo = ysf
        gyb = [B("gy", k) for k in range(8)]

        for kc in range(8):
            DMA("pool", gw[:, kc, :], gwin_d[:, kc, :], [], [B("gw", kc)])
        gwb = [B("gw", kc) for kc in range(8)]
        DMA("pool", ggo, ggo_d, [], [B("ggo")])
        for oc in range(8):
            DMA("pool", gwo[:, :, oc * 128:(oc + 1) * 128], gwo_d[oc], [], [B("gwo", oc)])
        DMA("sp", wg, gwg_d, [], [B("wg")])
        S.add("dve", lambda e: e.memset(zT, 0.0), writes=[B("zT")])
        S.add("dve", lambda e: e.memset(zT[32:64, :], 1.0), writes=[B("zT")])
        DMA("sp", tri, tri_d, [], [B("tri")])
        DMA("pool", msk, msk_d, [], [B("msk")])
        DMA("sp", sel, sel_d, [], [B("sel")])
        hb = [B("gh", k) for k in range(8)]
        sbc = {"n": 0, "g": 0}

        def seq_of(c):
            if c < 16:
                return 0, 0, 15
            if c < 18:
                return 1, 16, 17
            return 2, 18, 19

        def proj_tok(bank, col0, ncols, dst, dkey, func=None):
            MMG(ps[bank][:, 0:ncols], [(hc_[:, kc, :], gw[:, kc, col0:col0 + ncols]) for kc in range(8)], hb + gwb, [PB[bank]])
            if func is None:
                ACOPY(dst, ps[bank][:, 0:ncols], [PB[bank]], [B(dkey)])
            else:
                ACT(dst, ps[bank][:, 0:ncols], func, [PB[bank]], [B(dkey)])

        def proj_featT(bank, col0, dst, dkey):
            for hh in range(4):
                MMG(ps[bank][:, hh * 128:(hh + 1) * 128],
                    [(gw[:, kc, col0 + hh * 128:col0 + (hh + 1) * 128], hc_[:, kc, :]) for kc in range(8)], hb + gwb, [PB[bank]])
            ACOPY(dst, ps[bank], [PB[bank]], [B(dkey)])

        def gates(d, bank):
            MM1(ps[bank], zT, wg[:, d, :], True, True, [B("zT"), B("wg")], [PB[bank]])
            ACT(L[d], ps[bank], AF.Exp, [PB[bank]], [B("L")], scale=-1.0)
            ACT(L[d], L[d], AF.Ln, [B("L")], [B("L")], bias=1.0)

        def state_update(banks, decap, deckey, sbi):
            for hh in range(4):
                bk = banks[hh // 2]
                o = ps[bk][:, (hh % 2) * 256:(hh % 2) * 256 + 256]
                MM1(o, khat[:, hh * 128:(hh + 1) * 128], vtok[:, hh * 256:(hh + 1) * 256], True, True, [B("khat"), B("vtok")], [PB[bk]])
                STT(Sst[:, hh, :], Sst[:, hh, :], decap[:, hh:hh + 1], o, ALU.mult, ALU.add, [B("S", hh), B(deckey), PB[bk]], [B("S", hh)])
            ACOPY(Sb[sbi], Sst.rearrange("p h v -> p (h v)"), [B("S", hh) for hh in range(4)], [B("Sb", sbi)])

        def init_state(seq, phase):
            sbi = sbc["n"] % 2
            sbc["n"] += 1
            if phase == 1 and seq == 0:
                DMA("sp", Sst, st0_d, [], [B("S", hh) for hh in range(4)])
            elif phase == 2 and seq == 0:
                DMA("sp", ysf, cc1o_d[0:128, :], [B("cc1o")], gyb)
                TS(Sst.rearrange("p h v -> p (h v)"), ysf, sel[:, 0:1], ALU.mult, gyb + [B("sel")], [B("S", hh) for hh in range(4)])
                DMA("sp", ysf, cc1o_d[128:256, :], [B("cc1o")], gyb)
                STT(Sst.rearrange("p h v -> p (h v)"), ysf, sel[:, 1:2], Sst.rearrange("p h v -> p (h v)"), ALU.mult, ALU.add,
                    gyb + [B("sel")] + [B("S", hh) for hh in range(4)], [B("S", hh) for hh in range(4)])
            else:
                S.add("dve", lambda e: e.memset(Sst.rearrange("p h v -> p (h v)"), 0.0), writes=[B("S", hh) for hh in range(4)])
            ACOPY(Sb[sbi], Sst.rearrange("p h v -> p (h v)"), [B("S", hh) for hh in range(4)], [B("Sb", sbi)])
            return sbi

        cur_sb = None
        CUT = int(os.environ.get("MK_CUT", "99"))
        for c in range(int(os.environ.get("MK_NCH", "20"))):
            seq, c_first, c_last = seq_of(c)
            cond = 0 if seq == 0 else 1
            tok0 = c * 128
            if c == c_first:
                cur_sb = init_state(seq, 1)
            DMA("sp", stscr_d[c], Sb[cur_sb], [B("Sb", cur_sb)], [B("stscr")])
            if CUT < 1:
                continue
            prenorm(hc_, "gh", tok0, 128, 0, 1, cond)
            if CUT < 2:
                continue
            proj_tok(0, 512, 512, ksb, "ksb")
            proj_tok(1, 1024, 512, vtok[:, 0:512], "vtok")
            proj_tok(2, 1536, 512, vtok[:, 512:1024], "vtok")
            if CUT < 3:
                continue
            MMG(ps[3][0:32, 0:128], [(gw[:, kc, 3072:3104], hc_[:, kc, :]) for kc in range(8)], hb + gwb, [PB[3]])
            VCOPY(zT[0:32, :], ps[3][0:32, 0:128], [PB[3]], [B("zT")])
            if CUT < 4:
                continue
            gates(0, 4)
            if CUT < 5:
                continue
            MM1(ps[5], tri[:, 2, :], L[0], True, True, [B("tri"), B("L")], [PB[5]])
            ACT(Ex, ps[5], AF.Exp, [PB[5]], [B("Ei")])
            TT(khat, ksb, Ex, ALU.mult, [B("ksb"), B("Ei")], [B("khat")])
            if CUT < 6:
                continue
            for hh in range(4):
                MM1(ps[6][:, hh:hh + 1], L[0][:, hh * 128:(hh + 1) * 128], negcol[:, 0:1], True, True, [B("L"), B("negcol")], [PB[6]])
            ACT(dec, ps[6][:, 0:4], AF.Exp, [PB[6]], [B("dec")])
            if CUT < 7:
                continue
            sbi = sbc["n"] % 2
            sbc["n"] += 1
            state_update((0, 3), dec, "dec", sbi)
            cur_sb = sbi
            if c == c_last:
                if seq == 0:
                    DMA("sp", cc1i_d, Sst.rearrange("p h v -> p (h v)"), [B("S", hh) for hh in range(4)], [B("cc1i")])
                    if os.environ.get("MK_NOCC", "") != "1":
                        S.add("pool", lambda e: e.collective_compute("AllGather", ALU.bypass, replica_groups=RG, ins=[cc1i_d], outs=[cc1o_d]),
                              reads=[B("cc1i")], writes=[B("cc1o")], dma=True, inc=1)
                else:
                    DMA("sp", stout_d[seq - 1, 0], Sst, [B("S", hh) for hh in range(4)], [B("stout")], semkey="out")

        order = [19, 18, 17, 16] + list(range(15, -1, -1))
        if os.environ.get("MK_GLA", "") == "p1":
            order = []
        nsfc = {"n": 0}
        ctx = {}

        def front(c):
            seq, c_first, c_last = seq_of(c)
            cond = 0 if seq == 0 else 1
            tok0 = c * 128
            sfi = nsfc["n"] % 2
            nsfc["n"] += 1
            ctx[c] = (seq, c_first, c_last, cond, tok0, sfi)
            DMA("sp", SFb[sfi], stscr_d[c], [B("stscr")], [B("SFb", sfi)])
            prenorm(hc_, "gh", tok0, 128, 0, 1, cond)
            proj_featT(0, 0, qsT, "qsT")
            proj_featT(1, 512, ksT, "ksT")
            proj_tok(2, 512, 512, ksb, "ksb")
            proj_tok(3, 1024, 512, vtok[:, 0:512], "vtok")
            proj_tok(4, 1536, 512, vtok[:, 512:1024], "vtok")
            proj_tok(5, 2048, 512, rg[:, 0:512], "rg", AF.Silu)
            proj_tok(6, 2560, 512, rg[:, 512:1024], "rg", AF.Silu)
            MMG(ps[7][0:32, 0:128], [(gw[:, kc, 3072:3104], hc_[:, kc, :]) for kc in range(8)], hb + gwb, [PB[7]])
            VCOPY(zT[0:32, :], ps[7][0:32, 0:128], [PB[7]], [B("zT")])
            for d in (1, 0):
                gates(d, d)
                if d == 1:
                    MM1(ps[4], tri[:, 3, :], L[1], True, True, [B("tri"), B("L")], [PB[4]])
                    ACT(Ex, ps[4], AF.Exp, [PB[4]], [B("Ei")])
                    TT(khat, ksb, Ex, ALU.mult, [B("ksb"), B("Ei")], [B("khat")])
                bk = 2 + d
                for hh in range(4):
                    MM1(ps[bk][:, hh * 128:(hh + 1) * 128], L[d][:, hh * 128:(hh + 1) * 128], tri[:, d, :], True, True,
                        [B("L"), B("tri")], [PB[bk]])
                ACT(E, ps[bk], AF.Exp, [PB[bk]], [B("E")])
                ACT(Ei, ps[bk], AF.Exp, [PB[bk]], [B("Ei")], scale=-1.0)
                STT(qtil[d], qsT, 128.0 ** -0.5, E, ALU.mult, ALU.mult, [B("qsT"), B("E")], [B("qtil", d)])
                TT(ktil[d], ksT, Ei, ALU.mult, [B("ksT"), B("Ei")], [B("ktil", d)])
                if d == 1:
                    VCOPY(dec, E.rearrange("p (h i) -> p h i", i=128)[:, :, 0], [B("E")], [B("dec")])
            for d in range(2):
                bk = 5 + d
                for hh in range(4):
                    MM1(ps[bk][:, hh * 128:(hh + 1) * 128], ktil[d][:, hh * 128:(hh + 1) * 128], qtil[d][:, hh * 128:(hh + 1) * 128],
                        True, True, [B("ktil", d), B("qtil", d)], [PB[bk]])
                TT(Am[d].rearrange("p (h i) -> p h i", i=128), ps[bk].rearrange("p (h i) -> p h i", i=128),
                   msk[:, d, :].unsqueeze(1).broadcast_to([128, 4, 128]), ALU.mult, [PB[bk], B("msk")], [B("Am", d)])

        def mid(c):
            nonlocal cur_sb
            seq, c_first, c_last, cond, tok0, sfi = ctx[c]
            if c == c_last:
                cur_sb = init_state(seq, 2)
            for hh in range(4):
                bk = hh // 2
                o = ps[bk][:, (hh % 2) * 256:(hh % 2) * 256 + 256]
                hs = slice(hh * 128, (hh + 1) * 128)
                vs = slice(hh * 256, (hh + 1) * 256)
                MMG(o, [(Am[0][:, hs], vtok[:, vs]), (Am[1][:, hs], vtok[:, vs]), (qtil[0][:, hs], SFb[sfi][:, vs]), (qtil[1][:, hs], Sb[cur_sb][:, vs])],
                    [B("Am", 0), B("Am", 1), B("vtok"), B("qtil", 0), B("qtil", 1), B("SFb", sfi), B("Sb", cur_sb)], [PB[bk]])
            for bk in range(2):
                ACT(sqo[:, bk * 512:(bk + 1) * 512], ps[bk], AF.Square, [PB[bk]], gyb[bk * 4:bk * 4 + 4])
            S.add("dve", lambda e: e.reduce_sum(out=ssq, in_=sqo.rearrange("p (h v) -> p h v", v=256), axis=AX.X),
                  reads=gyb, writes=[B("ssq")])
            ACT(rso, ssq, AF.Ln, [B("ssq"), B("eps")], [B("rso")], bias=epsb[:, 0:1], scale=1.0 / 256.0)
            ACT(rso, rso, AF.Exp, [B("rso")], [B("rso")], scale=-0.5)
            for hh in range(4):
                bk = hh // 2
                o = ps[bk][:, (hh % 2) * 256:(hh % 2) * 256 + 256]
                vs = slice(hh * 256, (hh + 1) * 256)
                STT(ono, o, rso[:, hh:hh + 1], ggo[:, vs], ALU.mult, ALU.mult, [PB[bk], B("rso"), B("ggo")], [B("ono")])
                TT(og[:, vs], ono, rg[:, vs], ALU.mult, [B("ono"), B("rg")], [B("og")])
            sbi = sbc["n"] % 2
            sbc["n"] += 1
            state_update((2, 3), dec, "dec", sbi)
            cur_sb = sbi
            if c == c_first and seq > 0:
                DMA("sp", stout_d[seq - 1, 1], Sst, [B("S", hh) for hh in range(4)], [B("stout")], semkey="out")

        def tail(c):
            seq, c_first, c_last, cond, tok0, sfi = ctx[c]
            psTb = ps[4].bitcast(BF16)
            for j in range(8):
                S.add("pe", lambda e, j=j: e.transpose(out=psTb[:, j * 128:(j + 1) * 128], in_=og[:, j * 128:(j + 1) * 128], identity=identb),
                      reads=[B("og"), B("ident")], writes=[PB[4]])
            ACOPY(ogT.rearrange("p k t -> p (k t)"), psTb[:, 0:1024], [PB[4]], [B("ogT")])
            for oc in range(8):
                bk = 5 + oc // 4
                MMG(ps[bk][:, (oc % 4) * 128:(oc % 4 + 1) * 128], [(gwo[:, kc, oc * 128:(oc + 1) * 128], ogT[:, kc, :]) for kc in range(8)],
                    [B("gwo", oc), B("ogT")], [PB[bk]])
            for bk in range(2):
                ACOPY(ysf[:, bk * 512:(bk + 1) * 512],
                      ps[5 + bk], [PB[5 + bk]], [B("gy", k) for k in range(bk * 4, bk * 4 + 4)])
            postnorm_residual(ysb, "gy", tok0, 128, 0, 1, cond)

        prev_c = None
        for c in order:
            front(c)
            if prev_c is not None:
                tail(prev_c)
            mid(c)
            prev_c = c
        if prev_c is not None:
            tail(prev_c)
        barrier()
        A.reset(m)

    def mla():
        SC = 192.0 ** -0.5
        m = A.mark()
        cqn = A.alloc((4, NT), BF16)
        ckv_all = A.alloc((2, 4352), BF16)
        kr_all128 = A.alloc((4352,), BF16)
        kr_all = kr_all128[0:64]
        ckv_p = A.alloc((2, 512), BF16)
        kr_p128 = A.alloc((512,), BF16)
        kr_p = kr_p128[0:64]
        cs = A.alloc((512,), F32, parts=64)
        sn = A.alloc((512,), F32, parts=64)
        gq = A.alloc((4,), F32)
        gkv = A.alloc((2,), F32)
        mA = A.mark()
        mw = A.alloc((8, 896), BF16)
        h = A.alloc((8, 512), BF16)
        cq_sb = A.alloc((4, 512), F32)
        ckv_f = A.alloc((2, 512), F32)
        stg = A.alloc((2, 512), BF16)
        krst = A.alloc((512,), BF16, parts=64)
        krf = A.alloc((512,), F32, parts=64)
        t1 = A.alloc((512,), F32, parts=64)
        t2 = A.alloc((512,), F32, parts=64)
        cf = A.alloc((2, 256), F32)
        kf = A.alloc((256,), F32, parts=64)
        DMA("pool", mw, mwin_d, [], [B("mw")])
        DMA("sp", gq, mgq_d, [], [B("gq")])
        DMA("sp", gkv, mgkv_d, [], [B("gkv")])
        DMA("sp", cf, cckv_d, [], [B("cf")])
        DMA("sp", kf, ckr_d, [], [B("kf")])
        VCOPY(ckv_all[:, :, 0:256], cf, [B("cf")], [B("ckv_all")])
        S.add("dve", lambda e: e.memset(kr_all128[64:128, :], 0.0), writes=[B("kr_all")])
        S.add("dve", lambda e: e.memset(kr_p128[64:128, :], 0.0), writes=[B("kr_p")])
        VCOPY(kr_all[:, 0:256], kf, [B("kf")], [B("kr_all")])
        hb = [B("mh", k) for k in range(8)]
        cc2i_v = cc2i_d
        for t in range(5):
            c = 0 if t < 4 else 1
            tok0 = t * 512
            prenorm(h, "mh", tok0, 512, 1, 1, c)
            for j in range(4):
                MMG(ps[j % 2], [(mw[:, kc, j * 128:(j + 1) * 128], h[:, kc, :]) for kc in range(8)], hb + [B("mw")], [PB[j % 2]])
                ACOPY(cq_sb[:, j, :], ps[j % 2], [PB[j % 2]], [B("cq", j)])
            stats(lambda k: cq_sb[:, k, :], 4, 512, lambda k: [B("cq", k)], 2.0)
            for j in range(4):
                STT(cqn[:, j, tok0:tok0 + 512], cq_sb[:, j, :], gq[:, j:j + 1], rs_buf[:, 0:512], ALU.mult, ALU.mult,
                    [B("cq", j), B("gq"), B("rs")], [B("cqn", t)])
            for j in range(2):
                MMG(ps[2 + j], [(mw[:, kc, 512 + j * 128:512 + (j + 1) * 128], h[:, kc, :]) for kc in range(8)], hb + [B("mw")], [PB[2 + j]])
                ACOPY(ckv_f[:, j, :], ps[2 + j], [PB[2 + j]], [B("ckvf", j)])
            stats(lambda k: ckv_f[:, k, :], 2, 512, lambda k: [B("ckvf", k)], 4.0)
            for j in range(2):
                STT(ckv_f[:, j, :], ckv_f[:, j, :], gkv[:, j:j + 1], rs_buf[:, 0:512], ALU.mult, ALU.mult,
                    [B("ckvf", j), B("gkv"), B("rs")], [B("ckvf", j)])
            MMG(ps[4][0:64, :], [(mw[:, kc, 768:832], h[:, kc, :]) for kc in range(8)], hb + [B("mw")], [PB[4]])
            if t < 4:
                MMG(ps[5][0:64, :], [(mw[:, kc, 832:896], h[:, kc, :]) for kc in range(8)], hb + [B("mw")], [PB[5]])
                DMA("sp", cs, cos_d[:, tok0:tok0 + 512], [], [B("cs")])
                DMA("sp", sn, sin_d[:, tok0:tok0 + 512], [], [B("sn")])
                TT(t1, ps[4][0:64, :], cs, ALU.mult, [PB[4], B("cs")], [B("t1")])
                TT(t2, ps[5][0:64, :], sn, ALU.mult, [PB[5], B("sn")], [B("t2")])
                TT(krst, t1, t2, ALU.add, [B("t1"), B("t2")], [B("krst")])
                for j in range(2):
                    VCOPY(stg[:, j, :], ckv_f[:, j, :], [B("ckvf", j)], [B("stg")])
                    DMA("sp", cc2i_v[j * 128:(j + 1) * 128, tok0:tok0 + 512], stg[:, j, :], [B("stg")], [B("cc2i")])
                DMA("sp", cc2i_v[256:320, tok0:tok0 + 512], krst, [B("krst")], [B("cc2i")])
            else:
                ACOPY(krf, ps[4][0:64, :], [PB[4]], [B("krf")])
                DMA("sp", krout_d, krf, [B("krf")], [B("krout")], semkey="out")
                VCOPY(kr_p, krf, [B("krf")], [B("kr_p")])
                for j in range(2):
                    VCOPY(ckv_p[:, j, :], ckv_f[:, j, :], [B("ckvf", j)], [B("ckv_p")])
                DMA("sp", ckvout_d, ckv_f, [B("ckvf", 0), B("ckvf", 1)], [B("ckvout")], semkey="out")
        S.add("pool", lambda e: e.collective_compute("AllGather", ALU.bypass, replica_groups=RG, ins=[cc2i_d], outs=[cc2o_d]),
              reads=[B("cc2i")], writes=[B("cc2o")], dma=True, inc=1)
        for sl in range(2):
            for j in range(2):
                DMA("sp", ckv_all[:, j, 256 + sl * 2048:256 + (sl + 1) * 2048], cc2o_d[sl * 320 + j * 128:sl * 320 + (j + 1) * 128, :],
                    [B("cc2o")], [B("ckv_all")])
            DMA("sp", kr_all[:, 256 + sl * 2048:256 + (sl + 1) * 2048], cc2o_d[sl * 320 + 256:sl * 320 + 320, :], [B("cc2o")], [B("kr_all")])
        barrier()
        A.reset(mA)
        wq = [A.alloc((4, 256), BF16) for _ in range(2)]
        wkv = [A.alloc((2, 256), BF16) for _ in range(2)]
        KV = A.alloc((4352,), F32)
        Kh = KV[:, 0:2176].bitcast(BF16)
        Vh = KV[:, 2176:4352].bitcast(BF16).rearrange("p (a v) -> p a v", v=128)
        myb = [B("my", k) for k in range(8)]
        OT = A.alloc((16, 512), BF16)
        wos = [A.alloc((16, 128), BF16) for _ in range(2)]
        ysb = KV[:, 0:4096].rearrange("p (a t) -> p a t", t=512)
        PTbig = A.alloc((2048,), BF16)
        qn = [A.alloc((512,), BF16) for _ in range(2)]
        qr = [A.alloc((512,), BF16) for _ in range(2)]
        for qi_ in range(2):
            S.add("dve", lambda e, q=qr[qi_]: e.memset(q[64:128, :], 0.0), writes=[B("qr", qi_)])
        t1 = A.alloc((512,), F32, parts=64)
        t2 = A.alloc((512,), F32, parts=64)
        acc2 = A.alloc((1024,), F32)
        cnt = {"w": 0, "pt": 0, "kv": 0, "wo": 0, "o": 0}

        def load_head_w(hh, need_kv=True):
            wi = cnt["w"] % 2
            cnt["w"] += 1
            DMA("pool", wq[wi], wuq_d[hh], [], [B("wq", wi)])
            if need_kv:
                DMA("pool", wkv[wi], wukv_d[hh], [], [B("wkv", wi)])
            return wi

        GRP = [(0, 9), (9, 18), (18, 26), (26, 34)]

        def grp(ch):
            for g, (a, b) in enumerate(GRP):
                if a <= ch < b:
                    return g
            raise AssertionError(ch)

        def grps(c0, c1):
            return sorted(set(grp(c) for c in range(c0, c1)))

        allK = [B("Kh", g) for g in range(4)]
        allV = [B("Vh", g) for g in range(4)]
        Vflat = Vh.rearrange("p a v -> p (a v)")

        def kv_load_group(hh, g, which, alias):
            a, b = GRP[g]
            if which == "K":
                DMA("sp", Kh[:, a * 128:b * 128], kvc_d[hh, :, a * 128:b * 128], [B("kvc")], [B("Kh", g)] + alias)
            else:
                DMA("sp", Vflat[:, a * 128:b * 128], kvc_d[hh, :, 4352 + a * 128:4352 + b * 128], [B("kvc")], [B("Vh", g)] + alias)

        def kv_compute(hh, wi, ckv_src, ckvkey, nk, store):
            nch = nk // 128
            for k0 in range(0, nk, 512):
                n = min(512, nk - k0)
                bk = 6 + cnt["kv"] % 2
                cnt["kv"] += 1
                MMG(ps[bk][:, 0:n], [(wkv[wi][:, kc, 0:128], ckv_src[:, kc, k0:k0 + n]) for kc in range(2)], [B("wkv", wi), B(ckvkey)], [PB[bk]])
                wb = [B("Kh", g) for g in grps(k0 // 128, (k0 + n) // 128)] + myb
                if (k0 // 512) % 2 == 0:
                    VCOPY(Kh[:, k0:k0 + n], ps[bk][:, 0:n], [PB[bk]], wb)
                else:
                    ACOPY(Kh[:, k0:k0 + n], ps[bk][:, 0:n], [PB[bk]], wb)
            for g0 in range(0, nch, 4):
                g = min(4, nch - g0)
                bk = 6 + cnt["kv"] % 2
                cnt["kv"] += 1
                for i in range(g):
                    ch = g0 + i
                    MMG(ps[bk][:, i * 128:(i + 1) * 128], [(ckv_src[:, kc, ch * 128:(ch + 1) * 128], wkv[wi][:, kc, 128:256]) for kc in range(2)],
                        [B("wkv", wi), B(ckvkey)], [PB[bk]])
                wb = [B("Vh", gg) for gg in grps(g0, g0 + g)] + myb
                if (g0 // 4) % 2 == 0:
                    ACOPY(Vh[:, g0:g0 + g, :].rearrange("p a v -> p (a v)"), ps[bk][:, 0:g * 128], [PB[bk]], wb)
                else:
                    VCOPY(Vh[:, g0:g0 + g, :].rearrange("p a v -> p (a v)"), ps[bk][:, 0:g * 128], [PB[bk]], wb)
            if store:
                DMA("sp", kvc_d[hh, :, 0:nk], Kh[:, 0:nk], allK, [B("kvc")], semkey="kvc")
                DMA("sp", kvc_d[hh, :, 4352:4352 + nk], Vflat[:, 0:nk], allV, [B("kvc")], semkey="kvc")

        def qproj(wi, tok0, nq, rope, qi):
            cq_b = [B("cqn", t) for t in range(5)]
            bk = 6 + cnt["kv"] % 2
            cnt["kv"] += 1
            MMG(ps[bk][:, 0:nq], [(wq[wi][:, kc, 0:128], cqn[:, kc, tok0:tok0 + nq]) for kc in range(4)], [B("wq", wi)] + cq_b, [PB[bk]])
            ACOPY(qn[qi][:, 0:nq], ps[bk][:, 0:nq], [PB[bk]], [B("qn", qi)])
            bk = 6 + cnt["kv"] % 2
            cnt["kv"] += 1
            MMG(ps[bk][0:64, 0:nq], [(wq[wi][:, kc, 128:192], cqn[:, kc, tok0:tok0 + nq]) for kc in range(4)], [B("wq", wi)] + cq_b, [PB[bk]])
            if rope:
                bk2 = 6 + cnt["kv"] % 2
                cnt["kv"] += 1
                MMG(ps[bk2][0:64, 0:nq], [(wq[wi][:, kc, 192:256], cqn[:, kc, tok0:tok0 + nq]) for kc in range(4)], [B("wq", wi)] + cq_b, [PB[bk2]])
                TT(t1[:, 0:nq], ps[bk][0:64, 0:nq], cs[:, 0:nq], ALU.mult, [PB[bk], B("cs")], [B("t1")])
                TT(t2[:, 0:nq], ps[bk2][0:64, 0:nq], sn[:, 0:nq], ALU.mult, [PB[bk2], B("sn")], [B("t2")])
                TT(qr[qi][0:64, 0:nq], t1[:, 0:nq], t2[:, 0:nq], ALU.add, [B("t1"), B("t2")], [B("qr", qi)])
            else:
                VCOPY(qr[qi][0:64, 0:nq], ps[bk][0:64, 0:nq], [PB[bk]], [B("qr", qi)])

        def attn_loop(hh, kr_src, krkey, nk, nq, ot_col, qi, hook=None):
            nch = nk // 128
            assert nch % 2 == 0
            SB = (0, 1, 2, 4)
            LOOK = 3
            ob = (3, 5)[cnt["o"] % 2]
            cnt["o"] += 1
            slots = {}
            for ch in range(nch + LOOK):
                c2 = ch - LOOK
                extra = []
                if c2 >= 0:
                    sb2, pi = slots[c2]
                    pr_, half = pi // 2, pi % 2
                    pt = PTbig[:, pr_ * 1024 + half * 512:pr_ * 1024 + half * 512 + nq]
                    ACT(pt, ps[sb2][:, 0:nq], AF.Exp, [PB[sb2]], [B("PT", pi)], scale=SC)
                    extra = [B("PT", pi)]
                if ch < nch:
                    k = cnt["pt"]
                    cnt["pt"] += 1
                    sb = SB[k % 4]
                    slots[ch] = (sb, k % 4)
                    MMG(ps[sb][:, 0:nq], [(Kh[:, ch * 128:(ch + 1) * 128], qn[qi][:, 0:nq]), (kr_src[:, ch * 128:(ch + 1) * 128], qr[qi][:, 0:nq])],
                        [B("Kh", grp(ch)), B("qn", qi), B(krkey), B("qr", qi)] + extra + ([B("Vh", grp(c2))] if c2 >= 0 else []), [PB[sb]])
                    if hook:
                        hook("S", ch)
                if c2 >= 0:
                    MM1(ps[ob][:, 0:nq], Vh[:, c2, :], pt, c2 == 0, c2 == nch - 1, [B("Vh", grp(c2)), B("PT", pi)], [PB[ob]])
                    if half == 1:
                        if nq == 512:
                            src = PTbig[:, pr_ * 1024:(pr_ + 1) * 1024]
                            dst = acc2
                        else:
                            src = PTbig[:, pr_ * 1024:(pr_ + 1) * 1024].rearrange("p (h n) -> p h n", h=2)[:, :, 0:nq]
                            dst = acc2.rearrange("p (h n) -> p h n", h=2)[:, :, 0:nq]
                        if c2 == 1:
                            VCOPY(dst, src, [B("PT", pi - 1), B("PT", pi)], [B("acc")])
                        else:
                            TT(dst, dst, src, ALU.add, [B("acc"), B("PT", pi - 1), B("PT", pi)], [B("acc")])
                    if hook:
                        hook("PV", c2)
            bk = 6 + cnt["kv"] % 2
            cnt["kv"] += 1
            MM1(ps[bk][:, 0:nq], onesf, acc2[:, 0:nq], True, False, [B("onesf"), B("acc")], [PB[bk]])
            MM1(ps[bk][:, 0:nq], onesf, acc2[:, 512:512 + nq], False, True, [B("onesf"), B("acc")], [PB[bk]])
            ACT(acc2[:, 0:nq], ps[bk][:, 0:nq], AF.Ln, [PB[bk]], [B("acc")])
            ACT(acc2[:, 0:nq], acc2[:, 0:nq], AF.Exp, [B("acc")], [B("acc")], scale=-1.0)
            TT(OT[:, hh, ot_col:ot_col + nq], ps[ob][:, 0:nq], acc2[:, 0:nq], ALU.mult, [PB[ob], B("acc")], [B("OT", hh)])

        def outproj(tok0, c):
            for oc in range(8):
                wi = cnt["wo"] % 2
                cnt["wo"] += 1
                DMA("pool", wos[wi], mwo_d[oc], [], [B("wos", wi)])
                bk = 6 + wi
                MMG(ps[bk], [(wos[wi][:, hh, :], OT[:, hh, :]) for hh in range(16)], [B("wos", wi)] + [B("OT", hh) for hh in range(16)], [PB[bk]])
                ACOPY(ysb[:, oc, :], ps[bk], [PB[bk]], [B("my", oc)] + allK + allV)
            postnorm_residual(ysb, "my", tok0, 512, 1, 1, c)

        for qt in range(4):
            tok0 = qt * 512
            DMA("sp", cs, cos_d[:, tok0:tok0 + 512], [], [B("cs")])
            DMA("sp", sn, sin_d[:, tok0:tok0 + 512], [], [B("sn")])
            wi_cur = load_head_w(0, qt == 0)
            qproj(wi_cur, tok0, 512, True, 0)
            if qt > 0:
                for g in range(4):
                    kv_load_group(0, g, "K", myb)
                    kv_load_group(0, g, "V", myb)
            for hh in range(16):
                if qt == 0:
                    kv_compute(hh, wi_cur, ckv_all, "ckv_all", 4352, True)
                nxt = hh + 1 if hh < 15 else None
                st = {"wi": None}

                def hook(ev, idx, qt=qt, hh=hh, nxt=nxt, st=st, tok0=tok0):
                    if nxt is None:
                        return
                    if ev == "S" and idx == 2:
                        st["wi"] = load_head_w(nxt, qt == 0)
                    if ev == "S" and idx == 14:
                        qproj(st["wi"], tok0, 512, True, nxt % 2)
                    if qt > 0:
                        for g, (a, b) in enumerate(GRP):
                            if idx == b - 1:
                                kv_load_group(nxt, g, "K" if ev == "S" else "V", [])
                attn_loop(hh, kr_all128, "kr_all", 4352, 512, 0, hh % 2, hook)
                wi_cur = st["wi"]
            outproj(tok0, 0)
        for hh in range(16):
            wi = load_head_w(hh)
            for pr in range(2):
                kv_compute(hh, wi, ckv_p[:, :, pr * 256:(pr + 1) * 256], "ckv_p", 256, False)
                qproj(wi, NS + pr * 256, 256, False, pr)
                attn_loop(hh, kr_p128[:, pr * 256:(pr + 1) * 256], "kr_p", 256, 256, pr * 256, pr)
        outproj(NS, 1)
        barrier()
        A.reset(m)

    stages = ["ffn00", "gla", "ffn01", "ffn10", "mla", "ffn11"]
    fns = {"ffn00": lambda: ffn(0, 0), "gla": gla, "ffn01": lambda: ffn(0, 1), "ffn10": lambda: ffn(1, 0), "mla": mla, "ffn11": lambda: ffn(1, 1)}
    only = os.environ.get("MK_STAGES", "")
    for st in stages:
        if only and st not in only.split(","):
            continue
        fns[st]()
        if STAGE == st:
            break

    for k in range(8):
        DMA("sp", yT_d[:, k, :], xT[:, k, :], xb(k, 0, NT), [B("yT", k)], semkey="out")
    S.finalize()
    S.emit(nc, final_wait_keys=["out"])
    es.close()
    build_program.info = {"nops": len(S.ops), "arena_peak_words": A.peak, "nsem": len(S.final_counts)}
    return nc


def _rope_tables(pos):
    n_pairs = 16
    inv = (10000.0 ** (-np.arange(n_pairs, dtype=np.float32) / n_pairs)).astype(np.float32)
    row = (pos // 64).astype(np.float32)
    col = (pos % 64).astype(np.float32)
    ar = row[None, :] * inv[:, None]
    ac = col[None, :] * inv[:, None]
    cosT = np.concatenate([np.cos(ar), np.cos(ar), np.cos(ac), np.cos(ac)], 0).astype(np.float32)
    sinT = np.concatenate([-np.sin(ar), np.sin(ar), -np.sin(ac), np.sin(ac)], 0).astype(np.float32)
    return np.ascontiguousarray(cosT), np.ascontiguousarray(sinT)


def _featmajor(a):
    t, f = a.shape
    return np.ascontiguousarray(a.T.reshape(f // 128, 128, t).transpose(1, 0, 2))


def _wlay(w):
    k, n = w.shape
    return np.ascontiguousarray(w.reshape(k // 128, 128, n).transpose(1, 0, 2))


def kernel(x_prompt, x_sample, state_gla, cache_mla_ckv, cache_mla_krope, c, c_ctx,
           w_mod, b_mod, g_norm, w_ffn_in, w_ffn_out,
           gla_w_in, gla_w_gate, gla_b_gate, gla_g_out, gla_w_out,
           mla_w_in, mla_g_q, mla_g_kv, mla_w_uq, mla_w_ukv, mla_w_out):
    f = np.float32
    a = lambda v: np.asarray(v, dtype=f)
    x_prompt, x_sample, state_gla = a(x_prompt), a(x_sample), a(state_gla)
    cache_mla_ckv, cache_mla_krope, c, c_ctx = a(cache_mla_ckv), a(cache_mla_krope), a(c), a(c_ctx)
    w_mod, b_mod, g_norm, w_ffn_in, w_ffn_out = a(w_mod), a(b_mod), a(g_norm), a(w_ffn_in), a(w_ffn_out)
    gla_w_in, gla_w_gate, gla_b_gate, gla_g_out, gla_w_out = a(gla_w_in), a(gla_w_gate), a(gla_b_gate), a(gla_g_out), a(gla_w_out)
    mla_w_in, mla_g_q, mla_g_kv, mla_w_uq, mla_w_ukv, mla_w_out = a(mla_w_in), a(mla_g_q), a(mla_g_kv), a(mla_w_uq), a(mla_w_ukv), a(mla_w_out)

    wmod = np.ascontiguousarray(w_mod.reshape(2, 8, 128, 9, 1024).transpose(0, 3, 2, 1, 4))
    bmodT = np.ascontiguousarray(b_mod.reshape(2, 72, 128).transpose(2, 0, 1))
    gnT = np.ascontiguousarray(g_norm.reshape(2, 3, 2, 8, 128).transpose(4, 0, 1, 2, 3))
    w1 = w_ffn_in.reshape(2, 2, 8, 128, 2, NHC, 128)
    w1 = np.ascontiguousarray(w1.transpose(0, 1, 5, 3, 2, 4, 6)).reshape(2, 2, NHC, 128, 8, 256)
    w2 = w_ffn_out.reshape(2, 2, NHC, 128, 8, 128)
    w2 = np.ascontiguousarray(w2.transpose(0, 1, 4, 3, 2, 5))
    gwin = _wlay(gla_w_in[0])
    gwo = np.ascontiguousarray(gla_w_out[0].reshape(8, 128, 8, 128).transpose(2, 1, 0, 3))
    ggo = np.ascontiguousarray(np.broadcast_to(gla_g_out[0][None, :], (128, 1024)))
    jj = np.arange(128)[:, None]
    ii = np.arange(128)[None, :]
    tri = np.stack([(jj <= ii), (jj >= ii), (jj > ii), (jj < ii)], 1).astype(f) * f(-1.0 / 16.0)
    tri = np.ascontiguousarray(tri)
    mk = np.stack([(jj <= ii), (jj >= ii)], 1).astype(f)
    msk = np.ascontiguousarray(mk)
    ident = np.eye(128, dtype=f)
    perm = np.concatenate([np.arange(16, 32), np.arange(0, 16), np.arange(48, 64), np.arange(32, 48)])
    mw = mla_w_in[0]
    mwin = _wlay(np.concatenate([mw, mw[:, 768:832][:, perm]], 1))
    mgq = np.ascontiguousarray(mla_g_q[0].reshape(4, 128).T)
    mgkv = np.ascontiguousarray(mla_g_kv[0].reshape(2, 128).T)
    uq = mla_w_uq[0].reshape(512, 16, 192)
    uq = np.concatenate([uq, uq[:, :, 128:192][:, :, perm]], 2)
    wuq = np.ascontiguousarray(uq.reshape(4, 128, 16, 256).transpose(2, 1, 0, 3))
    ukv = mla_w_ukv[0].reshape(2, 128, 16, 256)
    wukv = np.ascontiguousarray(ukv.transpose(2, 1, 0, 3))
    mwo = np.ascontiguousarray(mla_w_out[0].reshape(16, 128, 8, 128).transpose(2, 1, 0, 3))

    in_maps = []
    for core in range(NCORES):
        p, r = core // 2, core % 2
        xs = x_sample[p, r * NS:(r + 1) * NS]
        pos = np.arange(r * NS, (r + 1) * NS)
        xp0, xp1 = x_prompt[2 * core], x_prompt[2 * core + 1]
        if r == 1:
            xs, pos, xp0, xp1 = xs[::-1], pos[::-1], xp0[::-1], xp1[::-1]
        xT = _featmajor(np.concatenate([xs, xp0, xp1], 0))
        condT = np.ascontiguousarray(np.stack([c[p], c_ctx], 1).reshape(8, 128, 2).transpose(1, 0, 2))
        gwg = np.zeros((128, 2, 512), f)
        for d in range(2):
            gd = d ^ r
            gwg[gd * 16:(gd + 1) * 16, d, :] = gla_w_gate[0, gd]
            gwg[32, d, :] = gla_b_gate[0, gd]
        st0 = np.ascontiguousarray(state_gla[p, 0, r].transpose(1, 0, 2))
        sel = np.zeros((128, 2), f)
        sel[:, 1 - r] = 1.0
        cosT, sinT = _rope_tables(pos)
        cckv = np.ascontiguousarray(cache_mla_ckv[p, 0].T.reshape(2, 128, 256).transpose(1, 0, 2))
        ckr = np.ascontiguousarray(cache_mla_krope[p, 0].T)
        in_maps.append(dict(xT=xT, condT=condT, wmod=wmod, bmodT=bmodT, gnT=gnT, w1=w1, w2=w2, gwin=gwin, gwg=gwg,
                            ggo=ggo, gwo=gwo, st0=st0, sel=sel, tri=tri, msk=msk, ident=ident, mwin=mwin, mgq=mgq, mgkv=mgkv,
                            wuq=wuq, wukv=wukv, mwo=mwo, cosT=cosT, sinT=sinT, cckv=cckv, ckr=ckr))

    nc = build_program()
    res = run_bass_kernel_spmd(nc, in_maps, core_ids=list(range(NCORES)))

    y_prompt = np.zeros((16, 256, D), f)
    y_sample = np.zeros((4, 4096, D), f)
    new_state = np.zeros((16, 1, 2, 4, 128, 256), f)
    new_ckv = np.zeros((16, 1, 256, 256), f)
    new_kr = np.zeros((16, 1, 256, 64), f)
    for core in range(NCORES):
        p, r = core // 2, core % 2
        o = res.results[core]
        y = np.asarray(o["yT"]).transpose(2, 1, 0).reshape(NT, D)
        ys, yp = y[:NS], [y[NS:NS + 256], y[NS + 256:]]
        ck = np.asarray(o["ckvout"]).transpose(2, 1, 0).reshape(512, 256)
        kr = np.asarray(o["krout"]).T
        cks, krs = [ck[:256], ck[256:]], [kr[:256], kr[256:]]
        if r == 1:
            ys = ys[::-1]
            yp = [v[::-1] for v in yp]
            cks = [v[::-1] for v in cks]
            krs = [v[::-1] for v in krs]
        y_sample[p, r * NS:(r + 1) * NS] = ys
        so = np.asarray(o["stout"])
        for pr in range(2):
            y_prompt[2 * core + pr] = yp[pr]
            new_ckv[2 * core + pr, 0] = cks[pr]
            new_kr[2 * core + pr, 0] = krs[pr]
            for d in range(2):
                new_state[2 * core + pr, 0, d ^ r] = so[pr, d].transpose(1, 0, 2)
    return (y_prompt, y_sample, new_state, new_ckv, new_kr)
```

```python
import os
from contextlib import ExitStack
import numpy as np
import concourse.bass as bass
import concourse.mybir as mybir
from concourse.bass_utils import run_bass_kernel_spmd

F32 = mybir.dt.float32
BF16 = mybir.dt.bfloat16
AF = mybir.ActivationFunctionType
ALU = mybir.AluOpType
AX = mybir.AxisListType

NCORES = 8
D = 1024
NT = 2560
NS = 2048
DFF = 2816
NHC = 22
EPS = 1e-6
ENGS = ("pe", "act", "dve", "pool", "sp")
STAGE = os.environ.get("MK_STAGE", "full")


class Buf:
    __slots__ = ("name", "last_writer", "readers")

    def __init__(self, name):
        self.name = name
        self.last_writer = None
        self.readers = []


class Op:
    __slots__ = ("eng", "fn", "dma", "ndma", "inc", "deps", "signal", "sigval", "semkey", "waits", "alldma", "snap")

    def __init__(self, eng, fn, dma, ndma, inc):
        self.eng = eng
        self.fn = fn
        self.dma = dma
        self.ndma = ndma
        self.inc = inc
        self.deps = {}
        self.signal = False
        self.sigval = None
        self.semkey = None
        self.waits = None
        self.alldma = False
        self.snap = None


class Sched:
    def __init__(self):
        self.ops = []
        self.bufs = {}
        self.nbar = 0

    def buf(self, *key):
        b = self.bufs.get(key)
        if b is None:
            b = Buf(key)
            self.bufs[key] = b
        return b

    def add(self, eng, fn, reads=(), writes=(), dma=False, ndma=1, inc=16, semkey=None):
        op = Op(eng, fn, dma, ndma, inc)
        for b in reads:
            w = b.last_writer
            if w is not None:
                op.deps[w] = True
        for b in writes:
            w = b.last_writer
            if w is not None and w not in op.deps:
                op.deps[w] = False
            for r in b.readers:
                if r is not op and r not in op.deps:
                    op.deps[r] = False
        for b in reads:
            b.readers.append(op)
        for b in writes:
            b.last_writer = op
            b.readers = []
        if dma:
            op.semkey = semkey if semkey is not None else ("dma", writes[0].name)
        self.ops.append(op)
        return op

    def finalize(self):
        for op in self.ops:
            need = []
            for d, raw in op.deps.items():
                if d.dma or op.dma:
                    need.append(d)
                elif d.eng != op.eng:
                    need.append(d)
                elif op.eng != "pe":
                    need.append(d)
            op.waits = need
            for d in need:
                d.signal = True
        counts = {}
        for op in self.ops:
            if op.alldma:
                op.snap = {k: v for k, v in counts.items() if k[0] != "eng"}
            if op.dma:
                k = op.semkey
                counts[k] = counts.get(k, 0) + op.inc * op.ndma
                op.sigval = counts[k]
                op.signal = True
            elif op.signal:
                k = ("eng", op.eng)
                op.semkey = k
                counts[k] = counts.get(k, 0) + 1
                op.sigval = counts[k]
        self.final_counts = counts

    def emit(self, nc, final_wait_keys=()):
        counts = self.final_counts
        keys = list(counts.keys())
        with ExitStack() as es:
            sems = {}
            for i, k in enumerate(keys):
                sems[k] = es.enter_context(nc.semaphore("s%d" % i))
            block = es.enter_context(nc.Block())
            by_eng = {e: [] for e in ENGS}
            for op in self.ops:
                by_eng[op.eng].append(op)

            def run(eng_name, eng):
                waited = {}
                for op in by_eng[eng_name]:
                    wmax = {}
                    for d in op.waits:
                        k = d.semkey
                        if d.sigval > wmax.get(k, 0):
                            wmax[k] = d.sigval
                    if op.snap:
                        for k, v in op.snap.items():
                            if v > wmax.get(k, 0):
                                wmax[k] = v
                    for k, v in wmax.items():
                        if waited.get(k, 0) >= v:
                            continue
                        eng.wait_ge(sems[k], v)
                        waited[k] = v
                    r = op.fn(eng)
                    if op.dma:
                        rl = r if isinstance(r, (list, tuple)) else [r]
                        assert len(rl) == op.ndma, (len(rl), op.ndma)
                        for ins in rl:
                            ins.then_inc(sems[op.semkey], op.inc)
                    elif op.signal:
                        r.then_inc(sems[op.semkey], 1)
                if eng_name == "sp":
                    for k in final_wait_keys:
                        if k in counts:
                            eng.wait_ge(sems[k], counts[k])

            @block.tensor
            def _(e):
                run("pe", e)

            @block.scalar
            def _(e):
                run("act", e)

            @block.vector
            def _(e):
                run("dve", e)

            @block.gpsimd
            def _(e):
                run("pool", e)

            @block.sync
            def _(e):
                run("sp", e)


class Arena:
    def __init__(self, t, nwords):
        self.t = t
        self.n = nwords
        self.off = 0
        self.peak = 0

    def mark(self):
        return self.off

    def reset(self, m):
        self.off = m

    def alloc(self, shape, dt, parts=128):
        n = int(np.prod(shape))
        words = (n + 1) // 2 if dt == BF16 else n
        words = (words + 7) // 8 * 8
        o = self.off
        self.off += words
        self.peak = max(self.peak, self.off)
        assert self.off <= self.n, ("arena overflow", self.off, self.n)
        v = self.t[0:parts, o:o + words]
        if dt == BF16:
            v = v.bitcast(BF16)
        v = v[:, 0:n]
        if len(shape) == 2:
            v = v.rearrange("p (a b) -> p a b", b=shape[1])
        elif len(shape) == 3:
            v = v.rearrange("p (a b c) -> p a b c", b=shape[1], c=shape[2])
        return v


def build_program():
    nc = bass.Bass("TRN2", target_bir_lowering=False)

    def din(name, shape, dt=F32):
        return nc.dram_tensor(name, list(shape), dt, kind="ExternalInput").ap()

    def dout(name, shape, dt=F32):
        return nc.dram_tensor(name, list(shape), dt, kind="ExternalOutput").ap()

    def dscr(name, shape, dt=F32):
        return nc.dram_tensor(name, list(shape), dt, kind="Internal").ap()

    xT_d = din("xT", [128, 8, NT])
    cond_d = din("condT", [128, 8, 2])
    wmod_d = din("wmod", [2, 9, 128, 8, 1024])
    bmod_d = din("bmodT", [128, 2, 72])
    gn_d = din("gnT", [128, 2, 3, 2, 8])
    w1_d = din("w1", [2, 2, NHC, 128, 8, 256])
    w2_d = din("w2", [2, 2, 8, 128, NHC, 128])
    gwin_d = din("gwin", [128, 8, 3104])
    gwg_d = din("gwg", [128, 2, 512])
    ggo_d = din("ggo", [128, 1024])
    gwo_d = din("gwo", [8, 128, 8, 128])
    st0_d = din("st0", [128, 4, 256])
    sel_d = din("sel", [128, 2])
    tri_d = din("tri", [128, 4, 128])
    msk_d = din("msk", [128, 2, 128])
    ident_d = din("ident", [128, 128])
    mwin_d = din("mwin", [128, 8, 896])
    mgq_d = din("mgq", [128, 4])
    mgkv_d = din("mgkv", [128, 2])
    wuq_d = din("wuq", [16, 128, 4, 256])
    wukv_d = din("wukv", [16, 128, 2, 256])
    mwo_d = din("mwo", [8, 128, 16, 128])
    cos_d = din("cosT", [64, NS])
    sin_d = din("sinT", [64, NS])
    cckv_d = din("cckv", [128, 2, 256])
    ckr_d = din("ckr", [64, 256])

    yT_d = dout("yT", [128, 8, NT])
    stout_d = dout("stout", [2, 2, 128, 4, 256])
    ckvout_d = dout("ckvout", [128, 2, 512])
    krout_d = dout("krout", [64, 512])

    stscr_d = dscr("stscr", [20, 128, 1024], BF16)
    cc1i_d = dscr("cc1i", [128, 1024])
    cc1o_d = dscr("cc1o", [256, 1024])
    cc2i_d = dscr("cc2i", [320, 2048], BF16)
    cc2o_d = dscr("cc2o", [640, 2048], BF16)
    kvc_d = dscr("kvc", [16, 128, 8704], BF16)
    RG = [[0, 1], [2, 3], [4, 5], [6, 7]]

    S = Sched()
    B = S.buf
    es = ExitStack()
    NW = 53200
    arena_t = es.enter_context(nc.sbuf_tensor("arena", [128, NW], F32))
    A = Arena(arena_t, NW)
    ps = [es.enter_context(nc.psum_tensor("ps%d" % i, [128, 512], F32))[:] for i in range(8)]
    PB = [B("ps", i) for i in range(8)]

    def DMA(eng, out, in_, reads, writes, semkey=None):
        S.add(eng, lambda e, o=out, i=in_: e.dma_start(out=o, in_=i), reads=reads, writes=writes, dma=True, semkey=semkey)

    def ACT(out, in_, func, reads, writes, **kw):
        S.add("act", lambda e, o=out, i=in_, f=func, k=kw: e.activation(out=o, in_=i, func=f, **k), reads=reads, writes=writes)

    def ACOPY(out, in_, reads, writes):
        S.add("act", lambda e, o=out, i=in_: e.copy(out=o, in_=i), reads=reads, writes=writes)

    def VCOPY(out, in_, reads, writes):
        S.add("dve", lambda e, o=out, i=in_: e.tensor_copy(out=o, in_=i), reads=reads, writes=writes)

    def TT(out, in0, in1, op, reads, writes):
        S.add("dve", lambda e, o=out, a=in0, b=in1, p=op: e.tensor_tensor(out=o, in0=a, in1=b, op=p), reads=reads, writes=writes)

    def STT(out, in0, scalar, in1, op0, op1, reads, writes):
        S.add("dve", lambda e, o=out, a=in0, s=scalar, b=in1, p0=op0, p1=op1:
              e.scalar_tensor_tensor(out=o, in0=a, scalar=s, in1=b, op0=p0, op1=p1), reads=reads, writes=writes)

    def TS(out, in0, s1, op0, reads, writes):
        S.add("dve", lambda e, o=out, a=in0, s=s1, p=op0: e.tensor_scalar(out=o, in0=a, scalar1=s, scalar2=None, op0=p), reads=reads, writes=writes)

    def MMG(out, pairs, reads, writes):
        def fn(e, o=out, pr=pairs):
            n = len(pr)
            ins = None
            for i, (l, r) in enumerate(pr):
                ins = e.matmul(o, lhsT=l, rhs=r, start=(i == 0), stop=(i == n - 1))
            return ins
        S.add("pe", fn, reads=reads, writes=writes)

    def MM1(out, l, r, start, stop, reads, writes):
        S.add("pe", lambda e, o=out, a=l, b=r, s0=start, s1=stop: e.matmul(o, lhsT=a, rhs=b, start=s0, stop=s1), reads=reads, writes=writes)

    def barrier():
        S.nbar += 1
        n = S.nbar
        S.add("pe", lambda e: e.matmul(ps[7][0:1, 0:1], lhsT=onesf[0:1, 0:1], rhs=onesf[0:1, 0:1], start=True, stop=True),
              reads=[B("onesf")], writes=[PB[7], B("bar", "pe")])
        S.add("act", lambda e: e.copy(out=bscr[:, 0:1], in_=bscr[:, 4:5]), writes=[B("bscr", 0), B("bar", "act")])
        S.add("dve", lambda e: e.tensor_copy(out=bscr[:, 1:2], in_=bscr[:, 5:6]), writes=[B("bscr", 1), B("bar", "dve")])
        S.add("pool", lambda e: e.tensor_copy(out=bscr[:, 2:3], in_=bscr[:, 6:7]), writes=[B("bscr", 2), B("bar", "pool")])
        allb = [B("bar", x) for x in ("pe", "act", "dve", "pool")]
        o = S.add("pe", lambda e: e.matmul(ps[7][0:1, 0:1], lhsT=onesf[0:1, 0:1], rhs=onesf[0:1, 0:1], start=True, stop=True),
                  reads=allb + [B("onesf")], writes=[PB[7]])
        o.alldma = True
        o = S.add("act", lambda e: e.copy(out=bscr[:, 0:1], in_=bscr[:, 4:5]), reads=allb, writes=[B("bscr", 0)])
        o.alldma = True
        o = S.add("dve", lambda e: e.tensor_copy(out=bscr[:, 1:2], in_=bscr[:, 5:6]), reads=allb, writes=[B("bscr", 1)])
        o.alldma = True
        o = S.add("pool", lambda e: e.tensor_copy(out=bscr[:, 2:3], in_=bscr[:, 6:7]), reads=allb, writes=[B("bscr", 2)])
        o.alldma = True
        o = S.add("sp", lambda e: None, reads=[], writes=[])
        o.deps = {b.last_writer: True for b in allb}
        o.alldma = True

    def xb(k, tok0, n):
        return [B("x", k, j) for j in range(tok0 // 128, (tok0 + n) // 128)]

    xT = A.alloc((8, NT), F32)
    onesM = A.alloc((128,), BF16)
    onesf = A.alloc((128,), F32)
    negcol = A.alloc((1,), F32)
    epsb = A.alloc((1,), F32)
    bscr = A.alloc((8,), F32)
    identb = A.alloc((128,), BF16)
    mods = [A.alloc((72, 2), F32) for _ in range(2)]
    Asc = [[A.alloc((8, 2), F32) for _ in range(3)] for _ in range(2)]
    Gsc = [[A.alloc((8, 2), F32) for _ in range(3)] for _ in range(2)]
    gn = A.alloc((2, 3, 2, 8), F32) if False else None
    gn = A.alloc((96,), F32).rearrange("p (l s j k) -> p l s j k", l=2, s=3, j=2, k=8)
    sq_all = A.alloc((1024,), BF16)
    sq_ring = [sq_all[:, 0:512], sq_all[:, 512:1024]]
    rs_buf = A.alloc((512,), F32)
    tmp_all = A.alloc((1024,), F32)
    tmp_ring = [tmp_all[:, 0:512], tmp_all[:, 512:1024]]
    PERSIST = A.mark()
    ctr = {"sq": 0, "tmp": 0}

    for k in range(8):
        DMA("sp", xT[:, k, :], xT_d[:, k, :], [], xb(k, 0, NT))
    S.add("dve", lambda e: e.memset(onesM, 1.0 / 1024.0), writes=[B("onesM")])
    S.add("dve", lambda e: e.memset(onesf, 1.0), writes=[B("onesf")])
    S.add("dve", lambda e: e.memset(negcol, -1.0 / 16.0), writes=[B("negcol")])
    S.add("dve", lambda e: e.memset(epsb, EPS), writes=[B("eps")])
    S.add("dve", lambda e: e.memset(bscr, 0.0), writes=[B("bscr", i) for i in range(4)])
    DMA("sp", gn.rearrange("p l s j k -> p (l s j k)"), gn_d.rearrange("p l s j k -> p (l s j k)"), [], [B("gn")])

    m0 = A.mark()
    cond_f = A.alloc((8, 2), F32)
    scb = A.alloc((8, 2), BF16)
    bmod = A.alloc((2, 72), F32)
    identf = A.alloc((128,), F32)
    wm_slots = [A.alloc((8, 1024), BF16) for _ in range(2)]
    DMA("sp", cond_f, cond_d, [], [B("cond_f")])
    DMA("sp", bmod, bmod_d, [], [B("bmod")])
    DMA("sp", identf, ident_d, [], [B("identf")])
    ACOPY(identb, identf, [B("identf")], [B("ident")])
    ACT(scb, cond_f, AF.Silu, [B("cond_f")], [B("scb")])
    psM = ps[6][:, 0:144].rearrange("p (a b) -> p a b", b=2)
    for l in range(2):
        for i in range(9):
            si = (l * 9 + i) % 2
            slot = wm_slots[si]
            DMA("pool", slot, wmod_d[l, i], [], [B("wm", si)])
            for jc in range(8):
                col = i * 8 + jc
                MMG(psM[:, col, :], [(slot[:, kc, jc * 128:(jc + 1) * 128], scb[:, kc, :]) for kc in range(8)],
                    [B("wm", si), B("scb")], [PB[6]])
        for c in range(2):
            TT(mods[l][:, :, c], psM[:, :, c], bmod[:, l, :], ALU.add, [PB[6], B("bmod")], [B("mods", l)])
        for s in range(3):
            w = 1.0 if s == 1 else 0.5
            for c in range(2):
                STT(Asc[l][s][:, :, c], mods[l][:, (3 * s + 1) * 8:(3 * s + 2) * 8, c], 1.0, gn[:, l, s, 0, :], ALU.add, ALU.mult,
                    [B("mods", l), B("gn")], [B("Asc", l, s)])
                STT(Gsc[l][s][:, :, c], mods[l][:, (3 * s + 2) * 8:(3 * s + 3) * 8, c], w, gn[:, l, s, 1, :], ALU.mult, ALU.mult,
                    [B("mods", l), B("gn")], [B("Gsc", l, s)])
    barrier()
    A.reset(m0)

    def stats_sq(src, srcbufs, n, slot=None, own=None):
        if own is not None:
            sq, sqb = own
        elif slot is not None:
            sq, sqb = sq_ring[slot], B("sq", slot)
        else:
            sq = sq_ring[ctr["sq"] % 2]
            sqb = B("sq", ctr["sq"] % 2)
            ctr["sq"] += 1
        ACT(sq[:, 0:n], src, AF.Square, srcbufs, [sqb])
        return sq, sqb

    def stats_mm(sq, sqb, n, bank, first, last):
        MM1(ps[bank][:, 0:n], onesM, sq[:, 0:n], first, last, [B("onesM"), sqb], [PB[bank]])

    def stats2(bank, n, scale, rs, rskey):
        ACT(rs[:, 0:n], ps[bank][:, 0:n], AF.Ln, [PB[bank], B("eps")], [B(*rskey)], bias=epsb[:, 0:1], scale=scale)
        ACT(rs[:, 0:n], rs[:, 0:n], AF.Exp, [B(*rskey)], [B(*rskey)], scale=-0.5)

    def stats(src_of_k, nk, n, srcbufs_of_k, scale):
        for k in range(nk):
            sq, sqb = stats_sq(src_of_k(k), srcbufs_of_k(k), n)
            stats_mm(sq, sqb, n, 7, k == 0, k == nk - 1)
        stats2(7, n, scale, rs_buf, ("rs",))

    def prenorm(h, hkey, tok0, n, l, s, c, rs=None, rskey=("rs",)):
        if rs is None:
            stats(lambda k: xT[:, k, tok0:tok0 + n], 8, n, lambda k: xb(k, tok0, n), 1.0)
            rs = rs_buf
        for k in range(8):
            ti = ctr["tmp"] % 2
            ctr["tmp"] += 1
            tmp = tmp_ring[ti]
            TT(tmp[:, 0:n], xT[:, k, tok0:tok0 + n], rs[:, 0:n], ALU.mult, xb(k, tok0, n) + [B(*rskey)], [B("tmp", ti)])
            ACT(h[:, k, 0:n], tmp[:, 0:n], AF.Identity, [B("tmp", ti), B("mods", l), B("Asc", l, s)],
                hkey(k) if callable(hkey) else [B(hkey, k)],
                bias=mods[l][:, 3 * s * 8 + k, c:c + 1], scale=Asc[l][s][:, k, c:c + 1])

    def prenorm128(h, hkey, tok0, l, s, c):
        xs = xT[:, :, tok0:tok0 + 128]
        xbs = [b for k in range(8) for b in xb(k, tok0, 128)]
        sqb = [B("sq", 0), B("sq", 1)]
        tb = [B("tmp", 0), B("tmp", 1)]
        ACT(sq_all.rearrange("p (k t) -> p k t", t=128), xs, AF.Square, xbs, sqb)
        for k in range(8):
            MM1(ps[7][:, 0:128], onesM, sq_all[:, k * 128:(k + 1) * 128], k == 0, k == 7, [B("onesM"), sqb[k // 4]], [PB[7]])
        stats2(7, 128, 1.0, rs_buf, ("rs",))
        TT(tmp_all.rearrange("p (k t) -> p k t", t=128), xs, rs_buf[:, 0:128].unsqueeze(1).broadcast_to([128, 8, 128]), ALU.mult,
           xbs + [B("rs")], tb)
        for k in range(8):
            ACT(h[:, k, 0:128], tmp_all[:, k * 128:(k + 1) * 128], AF.Identity, [tb[k // 4], B("mods", l), B("Asc", l, s)], [B(hkey, k)],
                bias=mods[l][:, 3 * s * 8 + k, c:c + 1], scale=Asc[l][s][:, k, c:c + 1])

    def postnorm128(ysb, ykey, tok0, l, s, c):
        yf = ysb.rearrange("p k t -> p (k t)")
        ybs = [B(ykey, k) for k in range(8)]
        sqb = [B("sq", 0), B("sq", 1)]
        tb = [B("tmp", 0), B("tmp", 1)]
        ACT(sq_all, yf, AF.Square, ybs, sqb)
        for k in range(8):
            MM1(ps[7][:, 0:128], onesM, sq_all[:, k * 128:(k + 1) * 128], k == 0, k == 7, [B("onesM"), sqb[k // 4]], [PB[7]])
        stats2(7, 128, 1.0, rs_buf, ("rs",))
        TT(tmp_all.rearrange("p (k t) -> p k t", t=128), ysb, rs_buf[:, 0:128].unsqueeze(1).broadcast_to([128, 8, 128]), ALU.mult,
           ybs + [B("rs")], tb)
        for k in range(8):
            STT(xT[:, k, tok0:tok0 + 128], tmp_all[:, k * 128:(k + 1) * 128], Gsc[l][s][:, k, c:c + 1], xT[:, k, tok0:tok0 + 128], ALU.mult, ALU.add,
                [tb[k // 4], B("Gsc", l, s)] + xb(k, tok0, 128), xb(k, tok0, 128))

    def postnorm_residual(ysb, ykey, tok0, n, l, s, c, stats_bank=None):
        yb = ykey if callable(ykey) else (lambda k: [B(ykey, k)])
        if stats_bank is None:
            stats(lambda k: ysb[:, k, 0:n], 8, n, yb, 1.0)
        else:
            stats2(stats_bank, n, 1.0, rs_buf, ("rs",))
        for k in range(8):
            ti = ctr["tmp"] % 2
            ctr["tmp"] += 1
            tmp = tmp_ring[ti]
            STT(tmp[:, 0:n], ysb[:, k, 0:n], Gsc[l][s][:, k, c:c + 1], rs_buf[:, 0:n], ALU.mult, ALU.mult,
                yb(k) + [B("Gsc", l, s), B("rs")], [B("tmp", ti)])
            S.add("pool", lambda e, o=xT[:, k, tok0:tok0 + n], t=tmp[:, 0:n]: e.tensor_tensor(out=o, in0=o, in1=t, op=ALU.add),
                  reads=xb(k, tok0, n) + [B("tmp", ti)], writes=xb(k, tok0, n))

    def ffn(l, f):
        s = 0 if f == 0 else 2
        m = A.mark()
        ysb = A.alloc((8, 1024), F32)
        h = A.alloc((8, 1024), BF16)
        act = A.alloc((NHC, 1024), BF16)
        w1s = [A.alloc((8, 256), BF16) for _ in range(2)]
        w2s = [A.alloc((NHC, 128), BF16) for _ in range(2)]
        sg = [A.alloc((512,), F32) for _ in range(2)]
        sq_post = (A.alloc((512,), BF16), B("sqp"))
        c1 = 0
        c2 = 0
        c3 = 0
        tiles = ((0, 1024, 0), (1024, 1024, 0), (2048, 512, 1))

        def pre_write_thunks(tok0, sub, c):
            out = []
            t0 = tok0 + sub * 512
            for k in range(8):
                def th(k=k, t0=t0, sub=sub, c=c):
                    ti = ctr["tmp"] % 2
                    ctr["tmp"] += 1
                    tmp = tmp_ring[ti]
                    TT(tmp, xT[:, k, t0:t0 + 512], sg[sub], ALU.mult, xb(k, t0, 512) + [B("sg", sub)], [B("tmp", ti)])
                    ACT(h[:, k, sub * 512:(sub + 1) * 512], tmp, AF.Identity, [B("tmp", ti), B("mods", l), B("Asc", l, s)], [B("h", k, sub)],
                        bias=mods[l][:, 3 * s * 8 + k, c:c + 1], scale=Asc[l][s][:, k, c:c + 1])
                out.append(th)
            return out

        def pre_stat_thunks(tok0, sub):
            out = []
            t0 = tok0 + sub * 512
            hold = {}
            for k in range(9):
                def th(k=k, t0=t0, sub=sub, hold=hold):
                    if k < 8:
                        hold[k] = stats_sq(xT[:, k, t0:t0 + 512], xb(k, t0, 512), 512, slot=k % 2)
                    if k >= 1:
                        sq, sqb = hold[k - 1]
                        stats_mm(sq, sqb, 512, sub, k - 1 == 0, k - 1 == 7)
                out.append(th)
            out.append(lambda sub=sub: stats2(sub, 512, 1.0, sg[sub], ("sg", sub)))
            return out

        def post_thunks(tok0, sub, c):
            out = []
            t0 = tok0 + sub * 512
            out.append(lambda sub=sub: stats2(6 + sub, 512, 1.0, rs_buf, ("rs",)))
            for k in range(8):
                def th(k=k, t0=t0, sub=sub, c=c):
                    ti = ctr["tmp"] % 2
                    ctr["tmp"] += 1
                    tmp = tmp_ring[ti]
                    STT(tmp, ysb[:, k, sub * 512:(sub + 1) * 512], Gsc[l][s][:, k, c:c + 1], rs_buf, ALU.mult, ALU.mult,
                        [B("y", k, sub), B("Gsc", l, s), B("rs")], [B("tmp", ti)])
                    TT(xT[:, k, t0:t0 + 512], xT[:, k, t0:t0 + 512], tmp, ALU.add, xb(k, t0, 512) + [B("tmp", ti)], xb(k, t0, 512))
                out.append(th)
            return out

        for sub in range(tiles[0][1] // 512):
            prenorm(h[:, :, sub * 512:(sub + 1) * 512], (lambda k, sub=sub: [B("h", k, sub)]), tiles[0][0] + sub * 512, 512, l, s, tiles[0][2])
        pend_post = []
        for ti_, (tok0, n, c) in enumerate(tiles):
            nsub = n // 512
            nxt = tiles[ti_ + 1] if ti_ + 1 < len(tiles) else None
            for hc in range(NHC):
                si = c1 % 2
                c1 += 1
                DMA("pool", w1s[si], w1_d[l, f, hc], [], [B("w1s", si)])
                for sub in range(nsub):
                    gi = c3 % 2
                    c3 += 1
                    hb = [B("h", k, sub) for k in range(8)]
                    hs = slice(sub * 512, (sub + 1) * 512)
                    MMG(ps[gi], [(w1s[si][:, kc, 0:128], h[:, kc, hs]) for kc in range(8)], [B("w1s", si)] + hb, [PB[gi], PB[2 + gi]])
                    MMG(ps[2 + gi], [(w1s[si][:, kc, 128:256], h[:, kc, hs]) for kc in range(8)], [B("w1s", si)] + hb, [PB[2 + gi]])
                    ACT(sg[gi], ps[gi], AF.Silu, [PB[gi]], [B("sg", gi)])
                    TT(act[:, hc, hs], sg[gi], ps[2 + gi], ALU.mult, [B("sg", gi), PB[2 + gi]], [B("act", hc, sub)])
                for _ in range(2):
                    if pend_post:
                        pend_post.pop(0)()
            while pend_post:
                pend_post.pop(0)()
            pre_work = []
            if nxt is not None:
                for sub in range(nxt[1] // 512):
                    pre_work += pre_stat_thunks(nxt[0], sub)
                for sub in range(nxt[1] // 512):
                    pre_work += pre_write_thunks(nxt[0], sub, nxt[2])
            nslots = 8 * nsub
            per_slot = -(-len(pre_work) // max(nslots - 1, 1)) if pre_work else 0
            lag = None
            for oc in range(8):
                si = c2 % 2
                c2 += 1
                DMA("pool", w2s[si], w2_d[l, f, oc], [], [B("w2s", si)])
                for sub in range(nsub):
                    bk = 4 + (c2 * 2 + sub) % 2
                    hs = slice(sub * 512, (sub + 1) * 512)
                    ab = [B("act", hc, sub) for hc in range(NHC)]
                    MMG(ps[bk], [(w2s[si][:, hc, :], act[:, hc, hs]) for hc in range(NHC)], [B("w2s", si)] + ab, [PB[bk]])
                    if lag is not None:
                        stats_mm(*lag)
                    ACOPY(ysb[:, oc, hs], ps[bk], [PB[bk]], [B("y", oc, sub)])
                    sq, sqb = stats_sq(ps[bk], [PB[bk]], 512, own=sq_post)
                    lag = (sq, sqb, 512, 6 + sub, oc == 0, oc == 7)
                    for _ in range(per_slot):
                        if pre_work:
                            pre_work.pop(0)()
            stats_mm(*lag)
            while pre_work:
                pre_work.pop(0)()
            for sub in range(nsub):
                pend_post += post_thunks(tok0, sub, c)
        while pend_post:
            pend_post.pop(0)()
        barrier()
        A.reset(m)

    def gla():
        m = A.mark()
        gw = A.alloc((8, 3104), BF16)
        gwo = A.alloc((8, 1024), BF16)
        ggo = A.alloc((1024,), BF16)
        wg = A.alloc((2, 512), F32)
        tri = A.alloc((4, 128), F32)
        msk = A.alloc((2, 128), BF16)
        sel = A.alloc((2,), F32)
        hc_ = A.alloc((8, 128), BF16)
        ksb = A.alloc((512,), BF16)
        ksT = A.alloc((512,), BF16)
        qsT = A.alloc((512,), BF16)
        vtok = A.alloc((1024,), BF16)
        rg = A.alloc((1024,), BF16)
        zT = A.alloc((128,), F32)
        Lb_ = A.alloc((512,), F32)
        L = [Lb_, Lb_]
        khat = A.alloc((512,), BF16)
        dec = A.alloc((4,), F32)
        Sst = A.alloc((4, 256), F32)
        Sb = [A.alloc((1024,), BF16) for _ in range(2)]
        SFb = [A.alloc((1024,), BF16) for _ in range(2)]
        E = A.alloc((512,), F32)
        Ei = A.alloc((512,), F32)
        Ex = Ei
        qtil = [A.alloc((512,), BF16) for _ in range(2)]
        ktil = [A.alloc((512,), BF16) for _ in range(2)]
        Am = [A.alloc((512,), BF16) for _ in range(2)]
        ssq = A.alloc((4,), F32)
        rso = A.alloc((4,), F32)
        ono = A.alloc((256,), BF16)
        og = A.alloc((1024,), BF16)
        ogT = A.alloc((8, 128), BF16)
        ysb = A.alloc((8, 128), F32)
        ysf = ysb.rearrange("p a t -> p (a t)")
        sqo = ysf
        gyb = [B("gy", k) for k in range(8)]

        for kc in range(8):
            DMA("pool", gw[:, kc, :], gwin_d[:, kc, :], [], [B("gw", kc)])
        gwb = [B("gw", kc) for kc in range(8)]
        DMA("pool", ggo, ggo_d, [], [B("ggo")])
        for oc in range(8):
            DMA("pool", gwo[:, :, oc * 128:(oc + 1) * 128], gwo_d[oc], [], [B("gwo", oc)])
        DMA("sp", wg, gwg_d, [], [B("wg")])
        S.add("dve", lambda e: e.memset(zT, 0.0), writes=[B("zT")])
        S.add("dve", lambda e: e.memset(zT[32:64, :], 1.0), writes=[B("zT")])
        DMA("sp", tri, tri_d, [], [B("tri")])
        DMA("pool", msk, msk_d, [], [B("msk")])
        DMA("sp", sel, sel_d, [], [B("sel")])
        hb = [B("gh", k) for k in range(8)]
        sbc = {"n": 0, "g": 0}

        def seq_of(c):
            if c < 16:
                return 0, 0, 15
            if c < 18:
                return 1, 16, 17
            return 2, 18, 19

        def proj_tok(bank, col0, ncols, dst, dkey, func=None):
            MMG(ps[bank][:, 0:ncols], [(hc_[:, kc, :], gw[:, kc, col0:col0 + ncols]) for kc in range(8)], hb + gwb, [PB[bank]])
            if func is None:
                ACOPY(dst, ps[bank][:, 0:ncols], [PB[bank]], [B(dkey)])
            else:
                ACT(dst, ps[bank][:, 0:ncols], func, [PB[bank]], [B(dkey)])

        def proj_featT(bank, col0, dst, dkey):
            for hh in range(4):
                MMG(ps[bank][:, hh * 128:(hh + 1) * 128],
                    [(gw[:, kc, col0 + hh * 128:col0 + (hh + 1) * 128], hc_[:, kc, :]) for kc in range(8)], hb + gwb, [PB[bank]])
            ACOPY(dst, ps[bank], [PB[bank]], [B(dkey)])

        def gates(d, bank):
            MM1(ps[bank], zT, wg[:, d, :], True, True, [B("zT"), B("wg")], [PB[bank]])
            ACT(L[d], ps[bank], AF.Exp, [PB[bank]], [B("L")], scale=-1.0)
            ACT(L[d], L[d], AF.Ln, [B("L")], [B("L")], bias=1.0)

        def state_update(banks, decap, deckey, sbi):
            for hh in range(4):
                bk = banks[hh // 2]
                o = ps[bk][:, (hh % 2) * 256:(hh % 2) * 256 + 256]
                MM1(o, khat[:, hh * 128:(hh + 1) * 128], vtok[:, hh * 256:(hh + 1) * 256], True, True, [B("khat"), B("vtok")], [PB[bk]])
                STT(Sst[:, hh, :], Sst[:, hh, :], decap[:, hh:hh + 1], o, ALU.mult, ALU.add, [B("S", hh), B(deckey), PB[bk]], [B("S", hh)])
            ACOPY(Sb[sbi], Sst.rearrange("p h v -> p (h v)"), [B("S", hh) for hh in range(4)], [B("Sb", sbi)])

        def init_state(seq, phase):
            sbi = sbc["n"] % 2
            sbc["n"] += 1
            if phase == 1 and seq == 0:
                DMA("sp", Sst, st0_d, [], [B("S", hh) for hh in range(4)])
            elif phase == 2 and seq == 0:
                DMA("sp", ysf, cc1o_d[0:128, :], [B("cc1o")], gyb)
                TS(Sst.rearrange("p h v -> p (h v)"), ysf, sel[:, 0:1], ALU.mult, gyb + [B("sel")], [B("S", hh) for hh in range(4)])
                DMA("sp", ysf, cc1o_d[128:256, :], [B("cc1o")], gyb)
                STT(Sst.rearrange("p h v -> p (h v)"), ysf, sel[:, 1:2], Sst.rearrange("p h v -> p (h v)"), ALU.mult, ALU.add,
                    gyb + [B("sel")] + [B("S", hh) for hh in range(4)], [B("S", hh) for hh in range(4)])
            else:
                S.add("dve", lambda e: e.memset(Sst.rearrange("p h v -> p (h v)"), 0.0), writes=[B("S", hh) for hh in range(4)])
            ACOPY(Sb[sbi], Sst.rearrange("p h v -> p (h v)"), [B("S", hh) for hh in range(4)], [B("Sb", sbi)])
            return sbi

        cur_sb = None
        CUT = int(os.environ.get("MK_CUT", "99"))
        for c in range(int(os.environ.get("MK_NCH", "20"))):
            seq, c_first, c_last = seq_of(c)
            cond = 0 if seq == 0 else 1
            tok0 = c * 128
            if c == c_first:
                cur_sb = init_state(seq, 1)
            DMA("sp", stscr_d[c], Sb[cur_sb], [B("Sb", cur_sb)], [B("stscr")])
            if CUT < 1:
                continue
            prenorm128(hc_, "gh", tok0, 0, 1, cond)
            if CUT < 2:
                continue
            proj_tok(0, 512, 512, ksb, "ksb")
            proj_tok(1, 1024, 512, vtok[:, 0:512], "vtok")
            proj_tok(2, 1536, 512, vtok[:, 512:1024], "vtok")
            if CUT < 3:
                continue
            MMG(ps[3][0:32, 0:128], [(gw[:, kc, 3072:3104], hc_[:, kc, :]) for kc in range(8)], hb + gwb, [PB[3]])
            VCOPY(zT[0:32, :], ps[3][0:32, 0:128], [PB[3]], [B("zT")])
            if CUT < 4:
                continue
            gates(0, 4)
            if CUT < 5:
                continue
            MM1(ps[5], tri[:, 2, :], L[0], True, True, [B("tri"), B("L")], [PB[5]])
            ACT(Ex, ps[5], AF.Exp, [PB[5]], [B("Ei")])
            TT(khat, ksb, Ex, ALU.mult, [B("ksb"), B("Ei")], [B("khat")])
            if CUT < 6:
                continue
            for hh in range(4):
                MM1(ps[6][:, hh:hh + 1], L[0][:, hh * 128:(hh + 1) * 128], negcol[:, 0:1], True, True, [B("L"), B("negcol")], [PB[6]])
            ACT(dec, ps[6][:, 0:4], AF.Exp, [PB[6]], [B("dec")])
            if CUT < 7:
                continue
            sbi = sbc["n"] % 2
            sbc["n"] += 1
            state_update((0, 3), dec, "dec", sbi)
            cur_sb = sbi
            if c == c_last:
                if seq == 0:
                    DMA("sp", cc1i_d, Sst.rearrange("p h v -> p (h v)"), [B("S", hh) for hh in range(4)], [B("cc1i")])
                    if os.environ.get("MK_NOCC", "") != "1":
                        S.add("pool", lambda e: e.collective_compute("AllGather", ALU.bypass, replica_groups=RG, ins=[cc1i_d], outs=[cc1o_d]),
                              reads=[B("cc1i")], writes=[B("cc1o")], dma=True, inc=1)
                else:
                    DMA("sp", stout_d[seq - 1, 0], Sst, [B("S", hh) for hh in range(4)], [B("stout")], semkey="out")

        order = [19, 18, 17, 16] + list(range(15, -1, -1))
        if os.environ.get("MK_GLA", "") == "p1":
            order = []
        nsfc = {"n": 0}
        ctx = {}

        def front(c):
            seq, c_first, c_last = seq_of(c)
            cond = 0 if seq == 0 else 1
            tok0 = c * 128
            sfi = nsfc["n"] % 2
            nsfc["n"] += 1
            ctx[c] = (seq, c_first, c_last, cond, tok0, sfi)
            DMA("sp", SFb[sfi], stscr_d[c], [B("stscr")], [B("SFb", sfi)])
            prenorm128(hc_, "gh", tok0, 0, 1, cond)
            proj_featT(0, 0, qsT, "qsT")
            proj_featT(1, 512, ksT, "ksT")
            proj_tok(2, 512, 512, ksb, "ksb")
            proj_tok(3, 1024, 512, vtok[:, 0:512], "vtok")
            proj_tok(4, 1536, 512, vtok[:, 512:1024], "vtok")
            proj_tok(5, 2048, 512, rg[:, 0:512], "rg", AF.Silu)
            proj_tok(6, 2560, 512, rg[:, 512:1024], "rg", AF.Silu)
            MMG(ps[7][0:32, 0:128], [(gw[:, kc, 3072:3104], hc_[:, kc, :]) for kc in range(8)], hb + gwb, [PB[7]])
            VCOPY(zT[0:32, :], ps[7][0:32, 0:128], [PB[7]], [B("zT")])
            for d in (1, 0):
                gates(d, d)
                if d == 1:
                    MM1(ps[4], tri[:, 3, :], L[1], True, True, [B("tri"), B("L")], [PB[4]])
                    ACT(Ex, ps[4], AF.Exp, [PB[4]], [B("Ei")])
                    TT(khat, ksb, Ex, ALU.mult, [B("ksb"), B("Ei")], [B("khat")])
                bk = 2 + d
                for hh in range(4):
                    MM1(ps[bk][:, hh * 128:(hh + 1) * 128], L[d][:, hh * 128:(hh + 1) * 128], tri[:, d, :], True, True,
                        [B("L"), B("tri")], [PB[bk]])
                ACT(E, ps[bk], AF.Exp, [PB[bk]], [B("E")])
                ACT(Ei, ps[bk], AF.Exp, [PB[bk]], [B("Ei")], scale=-1.0)
                STT(qtil[d], qsT, 128.0 ** -0.5, E, ALU.mult, ALU.mult, [B("qsT"), B("E")], [B("qtil", d)])
                TT(ktil[d], ksT, Ei, ALU.mult, [B("ksT"), B("Ei")], [B("ktil", d)])
                if d == 1:
                    VCOPY(dec, E.rearrange("p (h i) -> p h i", i=128)[:, :, 0], [B("E")], [B("dec")])
            for d in range(2):
                bk = 5 + d
                for hh in range(4):
                    MM1(ps[bk][:, hh * 128:(hh + 1) * 128], ktil[d][:, hh * 128:(hh + 1) * 128], qtil[d][:, hh * 128:(hh + 1) * 128],
                        True, True, [B("ktil", d), B("qtil", d)], [PB[bk]])
                TT(Am[d].rearrange("p (h i) -> p h i", i=128), ps[bk].rearrange("p (h i) -> p h i", i=128),
                   msk[:, d, :].unsqueeze(1).broadcast_to([128, 4, 128]), ALU.mult, [PB[bk], B("msk")], [B("Am", d)])

        def mid(c):
            nonlocal cur_sb
            seq, c_first, c_last, cond, tok0, sfi = ctx[c]
            if c == c_last:
                cur_sb = init_state(seq, 2)
            for hh in range(4):
                bk = hh // 2
                o = ps[bk][:, (hh % 2) * 256:(hh % 2) * 256 + 256]
                hs = slice(hh * 128, (hh + 1) * 128)
                vs = slice(hh * 256, (hh + 1) * 256)
                MMG(o, [(Am[0][:, hs], vtok[:, vs]), (Am[1][:, hs], vtok[:, vs]), (qtil[0][:, hs], SFb[sfi][:, vs]), (qtil[1][:, hs], Sb[cur_sb][:, vs])],
                    [B("Am", 0), B("Am", 1), B("vtok"), B("qtil", 0), B("qtil", 1), B("SFb", sfi), B("Sb", cur_sb)], [PB[bk]])
            for bk in range(2):
                ACT(sqo[:, bk * 512:(bk + 1) * 512], ps[bk], AF.Square, [PB[bk]], gyb[bk * 4:bk * 4 + 4])
            S.add("dve", lambda e: e.reduce_sum(out=ssq, in_=sqo.rearrange("p (h v) -> p h v", v=256), axis=AX.X),
                  reads=gyb, writes=[B("ssq")])
            ACT(rso, ssq, AF.Ln, [B("ssq"), B("eps")], [B("rso")], bias=epsb[:, 0:1], scale=1.0 / 256.0)
            ACT(rso, rso, AF.Exp, [B("rso")], [B("rso")], scale=-0.5)
            for hh in range(4):
                bk = hh // 2
                o = ps[bk][:, (hh % 2) * 256:(hh % 2) * 256 + 256]
                vs = slice(hh * 256, (hh + 1) * 256)
                STT(ono, o, rso[:, hh:hh + 1], ggo[:, vs], ALU.mult, ALU.mult, [PB[bk], B("rso"), B("ggo")], [B("ono")])
                TT(og[:, vs], ono, rg[:, vs], ALU.mult, [B("ono"), B("rg")], [B("og")])
            sbi = sbc["n"] % 2
            sbc["n"] += 1
            state_update((2, 3), dec, "dec", sbi)
            cur_sb = sbi
            if c == c_first and seq > 0:
                DMA("sp", stout_d[seq - 1, 1], Sst, [B("S", hh) for hh in range(4)], [B("stout")], semkey="out")

        def tail(c):
            seq, c_first, c_last, cond, tok0, sfi = ctx[c]
            psTb = ps[4].bitcast(BF16)
            for j in range(8):
                S.add("pe", lambda e, j=j: e.transpose(out=psTb[:, j * 128:(j + 1) * 128], in_=og[:, j * 128:(j + 1) * 128], identity=identb),
                      reads=[B("og"), B("ident")], writes=[PB[4]])
            ACOPY(ogT.rearrange("p k t -> p (k t)"), psTb[:, 0:1024], [PB[4]], [B("ogT")])
            for oc in range(8):
                bk = 5 + oc // 4
                MMG(ps[bk][:, (oc % 4) * 128:(oc % 4 + 1) * 128], [(gwo[:, kc, oc * 128:(oc + 1) * 128], ogT[:, kc, :]) for kc in range(8)],
                    [B("gwo", oc), B("ogT")], [PB[bk]])
            for bk in range(2):
                ACOPY(ysf[:, bk * 512:(bk + 1) * 512],
                      ps[5 + bk], [PB[5 + bk]], [B("gy", k) for k in range(bk * 4, bk * 4 + 4)])
            postnorm128(ysb, "gy", tok0, 0, 1, cond)

        prev_c = None
        for c in order:
            front(c)
            if prev_c is not None:
                tail(prev_c)
            mid(c)
            prev_c = c
        if prev_c is not None:
            tail(prev_c)
        barrier()
        A.reset(m)

    def mla():
        SC = 192.0 ** -0.5
        m = A.mark()
        cqn = A.alloc((4, NT), BF16)
        ckv_all = A.alloc((2, 4352), BF16)
        kr_all128 = A.alloc((4352,), BF16)
        kr_all = kr_all128[0:64]
        ckv_p = A.alloc((2, 512), BF16)
        kr_p128 = A.alloc((512,), BF16)
        kr_p = kr_p128[0:64]
        cs = A.alloc((512,), F32, parts=64)
        sn = A.alloc((512,), F32, parts=64)
        gq = A.alloc((4,), F32)
        gkv = A.alloc((2,), F32)
        mA = A.mark()
        mw = A.alloc((8, 896), BF16)
        h = A.alloc((8, 512), BF16)
        cq_sb = A.alloc((4, 512), F32)
        ckv_f = A.alloc((2, 512), F32)
        stg = A.alloc((2, 512), BF16)
        krst = A.alloc((512,), BF16, parts=64)
        krf = A.alloc((512,), F32, parts=64)
        t1 = A.alloc((512,), F32, parts=64)
        t2 = A.alloc((512,), F32, parts=64)
        cf = A.alloc((2, 256), F32)
        kf = A.alloc((256,), F32, parts=64)
        DMA("pool", mw, mwin_d, [], [B("mw")])
        DMA("sp", gq, mgq_d, [], [B("gq")])
        DMA("sp", gkv, mgkv_d, [], [B("gkv")])
        DMA("sp", cf, cckv_d, [], [B("cf")])
        DMA("sp", kf, ckr_d, [], [B("kf")])
        VCOPY(ckv_all[:, :, 0:256], cf, [B("cf")], [B("ckv_all")])
        S.add("dve", lambda e: e.memset(kr_all128[64:128, :], 0.0), writes=[B("kr_all")])
        S.add("dve", lambda e: e.memset(kr_p128[64:128, :], 0.0), writes=[B("kr_p")])
        VCOPY(kr_all[:, 0:256], kf, [B("kf")], [B("kr_all")])
        hb = [B("mh", k) for k in range(8)]
        cc2i_v = cc2i_d
        for t in range(5):
            c = 0 if t < 4 else 1
            tok0 = t * 512
            prenorm(h, "mh", tok0, 512, 1, 1, c)
            for j in range(4):
                MMG(ps[j % 2], [(mw[:, kc, j * 128:(j + 1) * 128], h[:, kc, :]) for kc in range(8)], hb + [B("mw")], [PB[j % 2]])
                ACOPY(cq_sb[:, j, :], ps[j % 2], [PB[j % 2]], [B("cq", j)])
            stats(lambda k: cq_sb[:, k, :], 4, 512, lambda k: [B("cq", k)], 2.0)
            for j in range(4):
                STT(cqn[:, j, tok0:tok0 + 512], cq_sb[:, j, :], gq[:, j:j + 1], rs_buf[:, 0:512], ALU.mult, ALU.mult,
                    [B("cq", j), B("gq"), B("rs")], [B("cqn", t)])
            for j in range(2):
                MMG(ps[2 + j], [(mw[:, kc, 512 + j * 128:512 + (j + 1) * 128], h[:, kc, :]) for kc in range(8)], hb + [B("mw")], [PB[2 + j]])
                ACOPY(ckv_f[:, j, :], ps[2 + j], [PB[2 + j]], [B("ckvf", j)])
            stats(lambda k: ckv_f[:, k, :], 2, 512, lambda k: [B("ckvf", k)], 4.0)
            for j in range(2):
                STT(ckv_f[:, j, :], ckv_f[:, j, :], gkv[:, j:j + 1], rs_buf[:, 0:512], ALU.mult, ALU.mult,
                    [B("ckvf", j), B("gkv"), B("rs")], [B("ckvf", j)])
            MMG(ps[4][0:64, :], [(mw[:, kc, 768:832], h[:, kc, :]) for kc in range(8)], hb + [B("mw")], [PB[4]])
            if t < 4:
                MMG(ps[5][0:64, :], [(mw[:, kc, 832:896], h[:, kc, :]) for kc in range(8)], hb + [B("mw")], [PB[5]])
                DMA("sp", cs, cos_d[:, tok0:tok0 + 512], [], [B("cs")])
                DMA("sp", sn, sin_d[:, tok0:tok0 + 512], [], [B("sn")])
                TT(t1, ps[4][0:64, :], cs, ALU.mult, [PB[4], B("cs")], [B("t1")])
                TT(t2, ps[5][0:64, :], sn, ALU.mult, [PB[5], B("sn")], [B("t2")])
                TT(krst, t1, t2, ALU.add, [B("t1"), B("t2")], [B("krst")])
                for j in range(2):
                    VCOPY(stg[:, j, :], ckv_f[:, j, :], [B("ckvf", j)], [B("stg")])
                    DMA("sp", cc2i_v[j * 128:(j + 1) * 128, tok0:tok0 + 512], stg[:, j, :], [B("stg")], [B("cc2i")])
                DMA("sp", cc2i_v[256:320, tok0:tok0 + 512], krst, [B("krst")], [B("cc2i")])
            else:
                ACOPY(krf, ps[4][0:64, :], [PB[4]], [B("krf")])
                DMA("sp", krout_d, krf, [B("krf")], [B("krout")], semkey="out")
                VCOPY(kr_p, krf, [B("krf")], [B("kr_p")])
                for j in range(2):
                    VCOPY(ckv_p[:, j, :], ckv_f[:, j, :], [B("ckvf", j)], [B("ckv_p")])
                DMA("sp", ckvout_d, ckv_f, [B("ckvf", 0), B("ckvf", 1)], [B("ckvout")], semkey="out")
        S.add("pool", lambda e: e.collective_compute("AllGather", ALU.bypass, replica_groups=RG, ins=[cc2i_d], outs=[cc2o_d]),
              reads=[B("cc2i")], writes=[B("cc2o")], dma=True, inc=1)
        for sl in range(2):
            for j in range(2):
                DMA("sp", ckv_all[:, j, 256 + sl * 2048:256 + (sl + 1) * 2048], cc2o_d[sl * 320 + j * 128:sl * 320 + (j + 1) * 128, :],
                    [B("cc2o")], [B("ckv_all")])
            DMA("sp", kr_all[:, 256 + sl * 2048:256 + (sl + 1) * 2048], cc2o_d[sl * 320 + 256:sl * 320 + 320, :], [B("cc2o")], [B("kr_all")])
        barrier()
        A.reset(mA)
        wq = [A.alloc((4, 256), BF16) for _ in range(2)]
        wkv = [A.alloc((2, 256), BF16) for _ in range(2)]
        KV = A.alloc((4352,), F32)
        Kh = KV[:, 0:2176].bitcast(BF16)
        Vh = KV[:, 2176:4352].bitcast(BF16).rearrange("p (a v) -> p a v", v=128)
        myb = [B("my", k) for k in range(8)]
        OT = A.alloc((16, 512), BF16)
        wos = [A.alloc((16, 128), BF16) for _ in range(2)]
        ysb = KV[:, 0:4096].rearrange("p (a t) -> p a t", t=512)
        PTbig = A.alloc((2048,), BF16)
        qn = [A.alloc((512,), BF16) for _ in range(2)]
        qr = [A.alloc((512,), BF16) for _ in range(2)]
        for qi_ in range(2):
            S.add("dve", lambda e, q=qr[qi_]: e.memset(q[64:128, :], 0.0), writes=[B("qr", qi_)])
        t1 = A.alloc((512,), F32, parts=64)
        t2 = A.alloc((512,), F32, parts=64)
        acc2 = A.alloc((1024,), F32)
        cnt = {"w": 0, "pt": 0, "kv": 0, "wo": 0, "o": 0}

        def load_head_w(hh, need_kv=True):
            wi = cnt["w"] % 2
            cnt["w"] += 1
            DMA("pool", wq[wi], wuq_d[hh], [], [B("wq", wi)])
            if need_kv:
                DMA("pool", wkv[wi], wukv_d[hh], [], [B("wkv", wi)])
            return wi

        GRP = [(0, 9), (9, 18), (18, 26), (26, 34)]

        def grp(ch):
            for g, (a, b) in enumerate(GRP):
                if a <= ch < b:
                    return g
            raise AssertionError(ch)

        def grps(c0, c1):
            return sorted(set(grp(c) for c in range(c0, c1)))

        allK = [B("Kh", g) for g in range(4)]
        allV = [B("Vh", g) for g in range(4)]
        Vflat = Vh.rearrange("p a v -> p (a v)")

        def kv_load_group(hh, g, which, alias):
            a, b = GRP[g]
            if which == "K":
                DMA("sp", Kh[:, a * 128:b * 128], kvc_d[hh, :, a * 128:b * 128], [B("kvc")], [B("Kh", g)] + alias)
            else:
                DMA("sp", Vflat[:, a * 128:b * 128], kvc_d[hh, :, 4352 + a * 128:4352 + b * 128], [B("kvc")], [B("Vh", g)] + alias)

        def kv_compute(hh, wi, ckv_src, ckvkey, nk, store):
            nch = nk // 128
            for k0 in range(0, nk, 512):
                n = min(512, nk - k0)
                bk = 6 + cnt["kv"] % 2
                cnt["kv"] += 1
                MMG(ps[bk][:, 0:n], [(wkv[wi][:, kc, 0:128], ckv_src[:, kc, k0:k0 + n]) for kc in range(2)], [B("wkv", wi), B(ckvkey)], [PB[bk]])
                wb = [B("Kh", g) for g in grps(k0 // 128, (k0 + n) // 128)] + myb
                if (k0 // 512) % 2 == 0:
                    VCOPY(Kh[:, k0:k0 + n], ps[bk][:, 0:n], [PB[bk]], wb)
                else:
                    ACOPY(Kh[:, k0:k0 + n], ps[bk][:, 0:n], [PB[bk]], wb)
            for g0 in range(0, nch, 4):
                g = min(4, nch - g0)
                bk = 6 + cnt["kv"] % 2
                cnt["kv"] += 1
                for i in range(g):
                    ch = g0 + i
                    MMG(ps[bk][:, i * 128:(i + 1) * 128], [(ckv_src[:, kc, ch * 128:(ch + 1) * 128], wkv[wi][:, kc, 128:256]) for kc in range(2)],
                        [B("wkv", wi), B(ckvkey)], [PB[bk]])
                wb = [B("Vh", gg) for gg in grps(g0, g0 + g)] + myb
                if (g0 // 4) % 2 == 0:
                    ACOPY(Vh[:, g0:g0 + g, :].rearrange("p a v -> p (a v)"), ps[bk][:, 0:g * 128], [PB[bk]], wb)
                else:
                    VCOPY(Vh[:, g0:g0 + g, :].rearrange("p a v -> p (a v)"), ps[bk][:, 0:g * 128], [PB[bk]], wb)
            if store:
                DMA("sp", kvc_d[hh, :, 0:nk], Kh[:, 0:nk], allK, [B("kvc")], semkey="kvc")
                DMA("sp", kvc_d[hh, :, 4352:4352 + nk], Vflat[:, 0:nk], allV, [B("kvc")], semkey="kvc")

        def qproj(wi, tok0, nq, rope, qi):
            cq_b = [B("cqn", t) for t in range(5)]
            bk = 6 + cnt["kv"] % 2
            cnt["kv"] += 1
            MMG(ps[bk][:, 0:nq], [(wq[wi][:, kc, 0:128], cqn[:, kc, tok0:tok0 + nq]) for kc in range(4)], [B("wq", wi)] + cq_b, [PB[bk]])
            ACOPY(qn[qi][:, 0:nq], ps[bk][:, 0:nq], [PB[bk]], [B("qn", qi)])
            bk = 6 + cnt["kv"] % 2
            cnt["kv"] += 1
            MMG(ps[bk][0:64, 0:nq], [(wq[wi][:, kc, 128:192], cqn[:, kc, tok0:tok0 + nq]) for kc in range(4)], [B("wq", wi)] + cq_b, [PB[bk]])
            if rope:
                bk2 = 6 + cnt["kv"] % 2
                cnt["kv"] += 1
                MMG(ps[bk2][0:64, 0:nq], [(wq[wi][:, kc, 192:256], cqn[:, kc, tok0:tok0 + nq]) for kc in range(4)], [B("wq", wi)] + cq_b, [PB[bk2]])
                TT(t1[:, 0:nq], ps[bk][0:64, 0:nq], cs[:, 0:nq], ALU.mult, [PB[bk], B("cs")], [B("t1")])
                TT(t2[:, 0:nq], ps[bk2][0:64, 0:nq], sn[:, 0:nq], ALU.mult, [PB[bk2], B("sn")], [B("t2")])
                TT(qr[qi][0:64, 0:nq], t1[:, 0:nq], t2[:, 0:nq], ALU.add, [B("t1"), B("t2")], [B("qr", qi)])
            else:
                VCOPY(qr[qi][0:64, 0:nq], ps[bk][0:64, 0:nq], [PB[bk]], [B("qr", qi)])

        def attn_loop(hh, kr_src, krkey, nk, nq, ot_col, qi, hook=None):
            nch = nk // 128
            assert nch % 2 == 0
            SB = (0, 1, 2, 4)
            LOOK = 3
            ob = (3, 5)[cnt["o"] % 2]
            cnt["o"] += 1
            slots = {}
            for ch in range(nch + LOOK):
                c2 = ch - LOOK
                extra = []
                if c2 >= 0:
                    sb2, pi = slots[c2]
                    pr_, half = pi // 2, pi % 2
                    pt = PTbig[:, pr_ * 1024 + half * 512:pr_ * 1024 + half * 512 + nq]
                    ACT(pt, ps[sb2][:, 0:nq], AF.Exp, [PB[sb2]], [B("PT", pi)], scale=SC)
                    extra = [B("PT", pi)]
                if ch < nch:
                    k = cnt["pt"]
                    cnt["pt"] += 1
                    sb = SB[k % 4]
                    slots[ch] = (sb, k % 4)
                    MMG(ps[sb][:, 0:nq], [(Kh[:, ch * 128:(ch + 1) * 128], qn[qi][:, 0:nq]), (kr_src[:, ch * 128:(ch + 1) * 128], qr[qi][:, 0:nq])],
                        [B("Kh", grp(ch)), B("qn", qi), B(krkey), B("qr", qi)] + extra + ([B("Vh", grp(c2))] if c2 >= 0 else []), [PB[sb]])
                    if hook:
                        hook("S", ch)
                if c2 >= 0:
                    MM1(ps[ob][:, 0:nq], Vh[:, c2, :], pt, c2 == 0, c2 == nch - 1, [B("Vh", grp(c2)), B("PT", pi)], [PB[ob]])
                    if half == 1:
                        if nq == 512:
                            src = PTbig[:, pr_ * 1024:(pr_ + 1) * 1024]
                            dst = acc2
                        else:
                            src = PTbig[:, pr_ * 1024:(pr_ + 1) * 1024].rearrange("p (h n) -> p h n", h=2)[:, :, 0:nq]
                            dst = acc2.rearrange("p (h n) -> p h n", h=2)[:, :, 0:nq]
                        if c2 == 1:
                            VCOPY(dst, src, [B("PT", pi - 1), B("PT", pi)], [B("acc")])
                        else:
                            TT(dst, dst, src, ALU.add, [B("acc"), B("PT", pi - 1), B("PT", pi)], [B("acc")])
                    if hook:
                        hook("PV", c2)
            bk = 6 + cnt["kv"] % 2
            cnt["kv"] += 1
            MM1(ps[bk][:, 0:nq], onesf, acc2[:, 0:nq], True, False, [B("onesf"), B("acc")], [PB[bk]])
            MM1(ps[bk][:, 0:nq], onesf, acc2[:, 512:512 + nq], False, True, [B("onesf"), B("acc")], [PB[bk]])
            ACT(acc2[:, 0:nq], ps[bk][:, 0:nq], AF.Ln, [PB[bk]], [B("acc")])
            ACT(acc2[:, 0:nq], acc2[:, 0:nq], AF.Exp, [B("acc")], [B("acc")], scale=-1.0)
            TT(OT[:, hh, ot_col:ot_col + nq], ps[ob][:, 0:nq], acc2[:, 0:nq], ALU.mult, [PB[ob], B("acc")], [B("OT", hh)])

        def outproj(tok0, c):
            for oc in range(8):
                wi = cnt["wo"] % 2
                cnt["wo"] += 1
                DMA("pool", wos[wi], mwo_d[oc], [], [B("wos", wi)])
                bk = 6 + wi
                MMG(ps[bk], [(wos[wi][:, hh, :], OT[:, hh, :]) for hh in range(16)], [B("wos", wi)] + [B("OT", hh) for hh in range(16)], [PB[bk]])
                ACOPY(ysb[:, oc, :], ps[bk], [PB[bk]], [B("my", oc)] + allK + allV)
            postnorm_residual(ysb, "my", tok0, 512, 1, 1, c)

        for qt in range(4):
            tok0 = qt * 512
            DMA("sp", cs, cos_d[:, tok0:tok0 + 512], [], [B("cs")])
            DMA("sp", sn, sin_d[:, tok0:tok0 + 512], [], [B("sn")])
            wi_cur = load_head_w(0, qt == 0)
            qproj(wi_cur, tok0, 512, True, 0)
            if qt > 0:
                for g in range(4):
                    kv_load_group(0, g, "K", myb)
                    kv_load_group(0, g, "V", myb)
            for hh in range(16):
                if qt == 0:
                    kv_compute(hh, wi_cur, ckv_all, "ckv_all", 4352, True)
                nxt = hh + 1 if hh < 15 else None
                st = {"wi": None}

                def hook(ev, idx, qt=qt, hh=hh, nxt=nxt, st=st, tok0=tok0):
                    if nxt is None:
                        return
                    if ev == "S" and idx == 2:
                        st["wi"] = load_head_w(nxt, qt == 0)
                    if ev == "S" and idx == 14:
                        qproj(st["wi"], tok0, 512, True, nxt % 2)
                    if qt > 0:
                        for g, (a, b) in enumerate(GRP):
                            if idx == b - 1:
                                kv_load_group(nxt, g, "K" if ev == "S" else "V", [])
                attn_loop(hh, kr_all128, "kr_all", 4352, 512, 0, hh % 2, hook)
                wi_cur = st["wi"]
            outproj(tok0, 0)
        for hh in range(16):
            wi = load_head_w(hh)
            for pr in range(2):
                kv_compute(hh, wi, ckv_p[:, :, pr * 256:(pr + 1) * 256], "ckv_p", 256, False)
                qproj(wi, NS + pr * 256, 256, False, pr)
                attn_loop(hh, kr_p128[:, pr * 256:(pr + 1) * 256], "kr_p", 256, 256, pr * 256, pr)
        outproj(NS, 1)
        barrier()
        A.reset(m)

    stages = ["ffn00", "gla", "ffn01", "ffn10", "mla", "ffn11"]
    fns = {"ffn00": lambda: ffn(0, 0), "gla": gla, "ffn01": lambda: ffn(0, 1), "ffn10": lambda: ffn(1, 0), "mla": mla, "ffn11": lambda: ffn(1, 1)}
    only = os.environ.get("MK_STAGES", "")
    for st in stages:
        if only and st not in only.split(","):
            continue
        fns[st]()
        if STAGE == st:
            break

    for k in range(8):
        DMA("sp", yT_d[:, k, :], xT[:, k, :], xb(k, 0, NT), [B("yT", k)], semkey="out")
    S.finalize()
    S.emit(nc, final_wait_keys=["out"])
    es.close()
    build_program.info = {"nops": len(S.ops), "arena_peak_words": A.peak, "nsem": len(S.final_counts)}
    return nc


def _rope_tables(pos):
    n_pairs = 16
    inv = (10000.0 ** (-np.arange(n_pairs, dtype=np.float32) / n_pairs)).astype(np.float32)
    row = (pos // 64).astype(np.float32)
    col = (pos % 64).astype(np.float32)
    ar = row[None, :] * inv[:, None]
    ac = col[None, :] * inv[:, None]
    cosT = np.concatenate([np.cos(ar), np.cos(ar), np.cos(ac), np.cos(ac)], 0).astype(np.float32)
    sinT = np.concatenate([-np.sin(ar), np.sin(ar), -np.sin(ac), np.sin(ac)], 0).astype(np.float32)
    return np.ascontiguousarray(cosT), np.ascontiguousarray(sinT)


def _featmajor(a):
    t, f = a.shape
    return np.ascontiguousarray(a.T.reshape(f // 128, 128, t).transpose(1, 0, 2))


def _wlay(w):
    k, n = w.shape
    return np.ascontiguousarray(w.reshape(k // 128, 128, n).transpose(1, 0, 2))


def kernel(x_prompt, x_sample, state_gla, cache_mla_ckv, cache_mla_krope, c, c_ctx,
           w_mod, b_mod, g_norm, w_ffn_in, w_ffn_out,
           gla_w_in, gla_w_gate, gla_b_gate, gla_g_out, gla_w_out,
           mla_w_in, mla_g_q, mla_g_kv, mla_w_uq, mla_w_ukv, mla_w_out):
    f = np.float32
    a = lambda v: np.asarray(v, dtype=f)
    x_prompt, x_sample, state_gla = a(x_prompt), a(x_sample), a(state_gla)
    cache_mla_ckv, cache_mla_krope, c, c_ctx = a(cache_mla_ckv), a(cache_mla_krope), a(c), a(c_ctx)
    w_mod, b_mod, g_norm, w_ffn_in, w_ffn_out = a(w_mod), a(b_mod), a(g_norm), a(w_ffn_in), a(w_ffn_out)
    gla_w_in, gla_w_gate, gla_b_gate, gla_g_out, gla_w_out = a(gla_w_in), a(gla_w_gate), a(gla_b_gate), a(gla_g_out), a(gla_w_out)
    mla_w_in, mla_g_q, mla_g_kv, mla_w_uq, mla_w_ukv, mla_w_out = a(mla_w_in), a(mla_g_q), a(mla_g_kv), a(mla_w_uq), a(mla_w_ukv), a(mla_w_out)

    wmod = np.ascontiguousarray(w_mod.reshape(2, 8, 128, 9, 1024).transpose(0, 3, 2, 1, 4))
    bmodT = np.ascontiguousarray(b_mod.reshape(2, 72, 128).transpose(2, 0, 1))
    gnT = np.ascontiguousarray(g_norm.reshape(2, 3, 2, 8, 128).transpose(4, 0, 1, 2, 3))
    w1 = w_ffn_in.reshape(2, 2, 8, 128, 2, NHC, 128)
    w1 = np.ascontiguousarray(w1.transpose(0, 1, 5, 3, 2, 4, 6)).reshape(2, 2, NHC, 128, 8, 256)
    w2 = w_ffn_out.reshape(2, 2, NHC, 128, 8, 128)
    w2 = np.ascontiguousarray(w2.transpose(0, 1, 4, 3, 2, 5))
    gwin = _wlay(gla_w_in[0])
    gwo = np.ascontiguousarray(gla_w_out[0].reshape(8, 128, 8, 128).transpose(2, 1, 0, 3))
    ggo = np.ascontiguousarray(np.broadcast_to(gla_g_out[0][None, :], (128, 1024)))
    jj = np.arange(128)[:, None]
    ii = np.arange(128)[None, :]
    tri = np.stack([(jj <= ii), (jj >= ii), (jj > ii), (jj < ii)], 1).astype(f) * f(-1.0 / 16.0)
    tri = np.ascontiguousarray(tri)
    mk = np.stack([(jj <= ii), (jj >= ii)], 1).astype(f)
    msk = np.ascontiguousarray(mk)
    ident = np.eye(128, dtype=f)
    perm = np.concatenate([np.arange(16, 32), np.arange(0, 16), np.arange(48, 64), np.arange(32, 48)])
    mw = mla_w_in[0]
    mwin = _wlay(np.concatenate([mw, mw[:, 768:832][:, perm]], 1))
    mgq = np.ascontiguousarray(mla_g_q[0].reshape(4, 128).T)
    mgkv = np.ascontiguousarray(mla_g_kv[0].reshape(2, 128).T)
    uq = mla_w_uq[0].reshape(512, 16, 192)
    uq = np.concatenate([uq, uq[:, :, 128:192][:, :, perm]], 2)
    wuq = np.ascontiguousarray(uq.reshape(4, 128, 16, 256).transpose(2, 1, 0, 3))
    ukv = mla_w_ukv[0].reshape(2, 128, 16, 256)
    wukv = np.ascontiguousarray(ukv.transpose(2, 1, 0, 3))
    mwo = np.ascontiguousarray(mla_w_out[0].reshape(16, 128, 8, 128).transpose(2, 1, 0, 3))

    in_maps = []
    for core in range(NCORES):
        p, r = core // 2, core % 2
        xs = x_sample[p, r * NS:(r + 1) * NS]
        pos = np.arange(r * NS, (r + 1) * NS)
        xp0, xp1 = x_prompt[2 * core], x_prompt[2 * core + 1]
        if r == 1:
            xs, pos, xp0, xp1 = xs[::-1], pos[::-1], xp0[::-1], xp1[::-1]
        xT = _featmajor(np.concatenate([xs, xp0, xp1], 0))
        condT = np.ascontiguousarray(np.stack([c[p], c_ctx], 1).reshape(8, 128, 2).transpose(1, 0, 2))
        gwg = np.zeros((128, 2, 512), f)
        for d in range(2):
            gd = d ^ r
            gwg[gd * 16:(gd + 1) * 16, d, :] = gla_w_gate[0, gd]
            gwg[32, d, :] = gla_b_gate[0, gd]
        st0 = np.ascontiguousarray(state_gla[p, 0, r].transpose(1, 0, 2))
        sel = np.zeros((128, 2), f)
        sel[:, 1 - r] = 1.0
        cosT, sinT = _rope_tables(pos)
        cckv = np.ascontiguousarray(cache_mla_ckv[p, 0].T.reshape(2, 128, 256).transpose(1, 0, 2))
        ckr = np.ascontiguousarray(cache_mla_krope[p, 0].T)
        in_maps.append(dict(xT=xT, condT=condT, wmod=wmod, bmodT=bmodT, gnT=gnT, w1=w1, w2=w2, gwin=gwin, gwg=gwg,
                            ggo=ggo, gwo=gwo, st0=st0, sel=sel, tri=tri, msk=msk, ident=ident, mwin=mwin, mgq=mgq, mgkv=mgkv,
                            wuq=wuq, wukv=wukv, mwo=mwo, cosT=cosT, sinT=sinT, cckv=cckv, ckr=ckr))

    nc = build_program()
    res = run_bass_kernel_spmd(nc, in_maps, core_ids=list(range(NCORES)))

    y_prompt = np.zeros((16, 256, D), f)
    y_sample = np.zeros((4, 4096, D), f)
    new_state = np.zeros((16, 1, 2, 4, 128, 256), f)
    new_ckv = np.zeros((16, 1, 256, 256), f)
    new_kr = np.zeros((16, 1, 256, 64), f)
    for core in range(NCORES):
        p, r = core // 2, core % 2
        o = res.results[core]
        y = np.asarray(o["yT"]).transpose(2, 1, 0).reshape(NT, D)
        ys, yp = y[:NS], [y[NS:NS + 256], y[NS + 256:]]
        ck = np.asarray(o["ckvout"]).transpose(2, 1, 0).reshape(512, 256)
        kr = np.asarray(o["krout"]).T
        cks, krs = [ck[:256], ck[256:]], [kr[:256], kr[256:]]
        if r == 1:
            ys = ys[::-1]
            yp = [v[::-1] for v in yp]
            cks = [v[::-1] for v in cks]
            krs = [v[::-1] for v in krs]
        y_sample[p, r * NS:(r + 1) * NS] = ys
        so = np.asarray(o["stout"])
        for pr in range(2):
            y_prompt[2 * core + pr] = yp[pr]
            new_ckv[2 * core + pr, 0] = cks[pr]
            new_kr[2 * core + pr, 0] = krs[pr]
            for d in range(2):
                new_state[2 * core + pr, 0, d ^ r] = so[pr, d].transpose(1, 0, 2)
    return (y_prompt, y_sample, new_state, new_ckv, new_kr)
```
